# Optimizing a Trainium2 kernel written in Bass

```python
import math
import jax, jax.numpy as jnp
from jax import lax
import numpy as np

D_MODEL = 1024
BATCH = 2
SEQ = 8192
DEPTH = 1

CONV_DIM = D_MODEL
CONV_WIDTH = 3
HEAD_DIM = 64
ATTN_SLOTS = 8
WINDOWS = (128, 512, 2048)
DILATIONS = (1, 4, 16)
N_GROUPS = len(WINDOWS)
ATTN_HEADS = N_GROUPS * ATTN_SLOTS
ATTN_DIM = ATTN_HEADS * HEAD_DIM
ATTN_OUT = ATTN_SLOTS * HEAD_DIM
Q_BLOCK = 128
EPS = 1e-6
NEG_INF = -1e30

IN_SIZES = (CONV_DIM, CONV_DIM, CONV_DIM, CONV_DIM,
            ATTN_DIM, ATTN_DIM, ATTN_DIM,
            ATTN_OUT,
            D_MODEL, D_MODEL)
IN_TOTAL = sum(IN_SIZES)
IN_SPLITS = tuple(int(v) for v in np.cumsum(IN_SIZES)[:-1])

kernel_name = "hybrid_shortconv_dilated_swa_gated_merge"


def rms_norm(t, w):
    tf = t.astype(jnp.float32)
    tf = tf * lax.rsqrt(jnp.mean(tf * tf, axis=-1, keepdims=True) + EPS)
    return (tf * w.astype(jnp.float32)).astype(t.dtype)


def window_attend(q, k, v, w_sub):
    n, length, hd = q.shape
    blk = math.gcd(Q_BLOCK, length)
    nblk = length // blk
    span = blk + w_sub
    kp = jnp.pad(k, ((0, 0), (w_sub, 0), (0, 0)))
    vp = jnp.pad(v, ((0, 0), (w_sub, 0), (0, 0)))
    idx = jnp.arange(nblk)[:, None] * blk + jnp.arange(span)[None, :]
    kb = kp[:, idx]
    vb = vp[:, idx]
    qb = q.reshape(n, nblk, blk, hd)
    scores = jnp.einsum('nbqd,nbkd->nbqk', qb, kb, preferred_element_type=jnp.float32) * (hd ** -0.5)
    dist = jnp.arange(blk)[:, None] + w_sub - jnp.arange(span)[None, :]
    valid = (dist >= 0) & (dist <= w_sub)
    valid = valid[None, :, :] & (idx >= w_sub)[:, None, :]
    scores = jnp.where(valid, scores, NEG_INF)
    mx = jnp.max(scores, axis=-1)
    p = jnp.exp(scores - mx[..., None])
    s = jnp.sum(p, axis=-1)
    o = jnp.einsum('nbqk,nbkd->nbqd', p, vb.astype(jnp.float32))
    return o.reshape(n, length, hd), mx.reshape(n, length), s.reshape(n, length)


def dilated_attend(q, k, v, window, dilation):
    b, s_len, h, hd = q.shape
    length = s_len // dilation
    w_sub = window // dilation

    def to_sub(t):
        return t.reshape(b, length, dilation, h, hd).transpose(0, 2, 3, 1, 4).reshape(b * dilation * h, length, hd)

    o, m, s = window_attend(to_sub(q), to_sub(k), to_sub(v), w_sub)
    o = o.reshape(b, dilation, h, length, hd).transpose(0, 3, 1, 2, 4).reshape(b, s_len, h, hd)
    m = m.reshape(b, dilation, h, length).transpose(0, 3, 1, 2).reshape(b, s_len, h)
    s = s.reshape(b, dilation, h, length).transpose(0, 3, 1, 2).reshape(b, s_len, h)
    return o, m, s


def setup_inputs(seed: int = 0) -> dict:
    key = jax.random.key(seed)
    ks = jax.random.split(key, 14)
    f32 = jnp.float32
    nrm = lambda k, shape: jax.random.normal(k, shape, f32)
    x = nrm(ks[0], (BATCH, SEQ, D_MODEL))
    c = nrm(ks[1], (BATCH, D_MODEL))
    w_ada = nrm(ks[2], (DEPTH, D_MODEL, 3 * D_MODEL)) * D_MODEL ** -0.5
    b_ada = 0.02 * nrm(ks[3], (DEPTH, 3 * D_MODEL))
    norm_w = 1.0 + 0.1 * nrm(ks[4], (DEPTH, D_MODEL))
    w_in = nrm(ks[5], (DEPTH, D_MODEL, IN_TOTAL)) * D_MODEL ** -0.5
    conv_w = nrm(ks[6], (DEPTH, CONV_WIDTH, CONV_DIM)) * CONV_WIDTH ** -0.5
    q_norm_w = 1.0 + 0.1 * nrm(ks[7], (DEPTH, HEAD_DIM))
    k_norm_w = 1.0 + 0.1 * nrm(ks[8], (DEPTH, HEAD_DIM))
    w_br_conv = nrm(ks[9], (DEPTH, CONV_DIM, D_MODEL)) * CONV_DIM ** -0.5
    w_br_attn = nrm(ks[10], (DEPTH, ATTN_OUT, D_MODEL)) * ATTN_OUT ** -0.5
    w_out = nrm(ks[11], (DEPTH, D_MODEL, D_MODEL)) * D_MODEL ** -0.5
    return {"x": x, "c": c, "w_ada": w_ada, "b_ada": b_ada, "norm_w": norm_w, "w_in": w_in,
            "conv_w": conv_w, "q_norm_w": q_norm_w, "k_norm_w": k_norm_w,
            "w_br_conv": w_br_conv, "w_br_attn": w_br_attn, "w_out": w_out}


def reference(x, c, w_ada, b_ada, norm_w, w_in, conv_w, q_norm_w, k_norm_w, w_br_conv, w_br_attn, w_out):
    b, s_len, _ = x.shape
    for l in range(DEPTH):
        mod = jax.nn.silu(c) @ w_ada[l] + b_ada[l]
        shift, scale, gate = jnp.split(mod[:, None, :], 3, axis=-1)
        h = rms_norm(x, norm_w[l]) * (1 + scale) + shift

        proj = h @ w_in[l]
        b_a, c_a, x_a, z_a, q, k, v, z_b, g_a, g_b = jnp.split(proj, IN_SPLITS, axis=-1)

        u = c_a * x_a
        conv = lax.conv_general_dilated(u, conv_w[l][:, None, :], window_strides=(1,),
                                        padding=[(CONV_WIDTH - 1, 0)],
                                        dimension_numbers=('NWC', 'WIO', 'NWC'),
                                        feature_group_count=CONV_DIM)
        y_a = b_a * conv * jax.nn.silu(z_a)

        q = rms_norm(q.reshape(b, s_len, N_GROUPS, ATTN_SLOTS, HEAD_DIM), q_norm_w[l])
        k = rms_norm(k.reshape(b, s_len, N_GROUPS, ATTN_SLOTS, HEAD_DIM), k_norm_w[l])
        v = v.reshape(b, s_len, N_GROUPS, ATTN_SLOTS, HEAD_DIM)
        outs, maxes, sums = [], [], []
        for g in range(N_GROUPS):
            o_g, m_g, s_g = dilated_attend(q[:, :, g], k[:, :, g], v[:, :, g], WINDOWS[g], DILATIONS[g])
            outs.append(o_g)
            maxes.append(m_g)
            sums.append(s_g)
        o_all = jnp.stack(outs, axis=0)
        m_all = jnp.stack(maxes, axis=0)
        s_all = jnp.stack(sums, axis=0)
        wgt = jnp.exp(m_all - jnp.max(m_all, axis=0, keepdims=True))
        attn = jnp.sum(wgt[..., None] * o_all, axis=0) / jnp.sum(wgt * s_all, axis=0)[..., None]
        y_b = attn.reshape(b, s_len, ATTN_OUT).astype(x.dtype) * jax.nn.silu(z_b)

        merged = jax.nn.sigmoid(g_a) * (y_a @ w_br_conv[l]) + jax.nn.sigmoid(g_b) * (y_b @ w_br_attn[l])
        x = x + gate * (merged @ w_out[l])
    return x
```

```python
import numpy as np
from contextlib import ExitStack
import concourse.bass as bass
import concourse.mybir as mybir
from concourse.bass_utils import run_bass_kernel_spmd

F32 = mybir.dt.float32
BF16 = mybir.dt.bfloat16
AF = mybir.ActivationFunctionType
ALU = mybir.AluOpType
AX = mybir.AxisListType

ENGS = ("pe", "act", "dve", "pool", "sp")
NCORES = 8
TOK = 2048
EPS = 1e-6
NCST = 320 + 9 * 512
M_T0, M_T0H, M_T1, M_T2, M_T2H, M_T3 = 0, 1, 2, 3, 4, 5
DEBUG = False
STOP = 9
SUB = 9
NRUN = NCORES


class Tok:
    __slots__ = ("name", "w", "r", "wd", "rd", "disjoint", "psum")

    def __init__(self, name="", disjoint=False, psum=False):
        self.name = name
        self.psum = psum
        self.w = {}
        self.r = {}
        self.wd = []
        self.rd = []
        self.disjoint = disjoint


class Op:
    __slots__ = ("eng", "fn", "deps", "dma", "semkey", "ticket", "sig")


class Sched:
    def __init__(self, nc, es, n_dma_sems=28):
        self.nc = nc
        self.sems = {}
        for e in ENGS:
            self.sems[e] = es.enter_context(nc.semaphore("s_" + e))
        self.ndma = n_dma_sems
        for i in range(n_dma_sems):
            self.sems[("d", i)] = es.enter_context(nc.semaphore("s_d%d" % i))
        self.semval = {k: 0 for k in self.sems}
        self.known = {e: {} for e in ENGS}
        self.dma_rr = 0
        self.dma_last = {}
        self.toks = []
        self.ops = {e: [] for e in ENGS}
        self.allops = []

    def tok(self, name="", disjoint=False, psum=False):
        t = Tok(name, disjoint, psum)
        self.toks.append(t)
        return t

    def add(self, eng, fn, reads=(), writes=(), dma=False):
        op = Op()
        op.eng = eng
        op.fn = fn
        op.dma = dma
        op.sig = False
        op.ticket = None
        deps = set()
        for t in reads:
            deps.update(t.w.values())
            deps.update(t.wd)
            if t.psum:
                deps.update(o for en, o in t.r.items() if en != eng)
        for t in writes:
            deps.update(t.r.values())
            deps.update(t.rd)
            if not t.disjoint:
                deps.update(t.w.values())
                deps.update(t.wd)
        if eng == "pe" and not dma:
            deps = {d for d in deps if not (d.eng == "pe" and not d.dma)}
        if dma:
            i = self.dma_rr
            self.dma_rr = (self.dma_rr + 1) % self.ndma
            op.semkey = ("d", i)
            prev = self.dma_last.get(i)
            if prev is not None:
                deps.add(prev)
            self.dma_last[i] = op
        else:
            op.semkey = eng
        deps.discard(op)
        op.deps = deps
        for d in deps:
            d.sig = True
        for t in reads:
            if dma:
                t.rd.append(op)
            else:
                t.r[eng] = op
        for t in writes:
            if not t.disjoint:
                t.w = {}
                t.wd = []
                t.r = {}
                t.rd = []
            if dma:
                t.wd.append(op)
            else:
                t.w[eng] = op
        self.ops[eng].append(op)
        self.allops.append(op)
        return op

    def wait_all(self, eng, ops):
        op = Op()
        op.eng = eng
        op.fn = None
        op.dma = False
        op.sig = False
        op.ticket = None
        op.semkey = eng
        op.deps = set(o for o in ops if o is not None)
        for d in op.deps:
            d.sig = True
        self.ops[eng].append(op)
        return op

    def emit(self, final=False):
        nc = self.nc
        for e in ENGS:
            pend = [o for o in self.ops[e] if o.dma]
            if pend:
                self.wait_all(e, pend)
        for op in self.allops:
            if op.sig and op.fn is not None:
                self.semval[op.semkey] += 16 if op.dma else 1
                op.ticket = self.semval[op.semkey]
        sched = self

        def run(ename, eng):
            known = sched.known[ename]
            for op in sched.ops[ename]:
                need = {}
                for d in op.deps:
                    assert d.ticket is not None, (ename, d.eng)
                    if need.get(d.semkey, 0) < d.ticket:
                        need[d.semkey] = d.ticket
                for k, v in need.items():
                    if known.get(k, 0) < v:
                        eng.wait_ge(sched.sems[k], v)
                        known[k] = v
                if op.fn is None:
                    continue
                ins = op.fn(eng)
                if op.sig:
                    ins.then_inc(sched.sems[op.semkey], 16 if op.dma else 1)

        with nc.Block(no_gpsimd_drain=True) as block:
            @block.tensor
            def _(e):
                run("pe", e)

            @block.scalar
            def _(e):
                run("act", e)

            @block.vector
            def _(e):
                run("dve", e)

            @block.gpsimd
            def _(e):
                run("pool", e)

            @block.sync
            def _(e):
                run("sp", e)
        for t in self.toks:
            t.w = {}
            t.r = {}
            t.wd = []
            t.rd = []
        self.toks = []
        self.ops = {e: [] for e in ENGS}
        self.allops = []


def build_program():
    nc = bass.Bass("TRN2", target_bir_lowering=False)
    dt = nc.dram_tensor
    x_d = dt("x", [32, 128, 1024], F32, kind="ExternalInput").ap()
    ccol_d = dt("ccol", [128, 8], F32, kind="ExternalInput").ap()
    cst_d = dt("cst", [128, NCST], F32, kind="ExternalInput").ap()
    small_d = dt("small", [128, 32], F32, kind="ExternalInput").ap()
    rows_d = dt("rows", [1, 4224], F32, kind="ExternalInput").ap()
    wada_d = dt("wada", [24, 128, 8, 128], F32, kind="ExternalInput").ap()
    win_d = dt("win", [88, 128, 8, 128], F32, kind="ExternalInput").ap()
    wbc_d = dt("wbc", [8, 128, 8, 128], F32, kind="ExternalInput").ap()
    wba_d = dt("wba", [8, 128, 4, 128], F32, kind="ExternalInput").ap()
    wout_d = dt("wout", [1024, 1024], F32, kind="ExternalInput").ap()
    out_d = dt("out", [16, 128, 1024], F32, kind="ExternalOutput").ap()
    if DEBUG:
        d_hT = dt("d_hT", [128, 8, TOK], BF16, kind="ExternalOutput").ap()
        d_hh = dt("d_hh", [128, 8, TOK], BF16, kind="ExternalOutput").ap()
        d_acsh = dt("d_acsh", [128, 16], F32, kind="ExternalOutput").ap()
        d_gate = dt("d_gate", [128, 1024], F32, kind="ExternalOutput").ap()
        d_negc = dt("d_negc", [128, 1], F32, kind="ExternalOutput").ap()
        d_yb = dt("d_yb", [128, 4, TOK], BF16, kind="ExternalOutput").ap()
        d_ya = dt("d_ya", [128, 8, TOK], BF16, kind="ExternalOutput").ap()
        d_mg = dt("d_mg", [128, 8, TOK], BF16, kind="ExternalOutput").ap()
        d_Q = [[dt("d_Q%d_%d" % (g, hd), [128, TOK], BF16, kind="ExternalOutput").ap() for hd in range(2)] for g in range(3)]
        d_K = [dt("d_K%d" % g, [128, (128, 512, 2048)[g] + TOK], BF16, kind="ExternalOutput").ap() for g in range(3)]
        d_V = [dt("d_V%d" % g, [128, (1, 4, 16)[g] + 16, 128], BF16, kind="ExternalOutput").ap() for g in range(3)]
        d_zs = dt("d_zs", [128, TOK], BF16, kind="ExternalOutput").ap()

    with ExitStack() as es:
        S = Sched(nc, es)

        def sbt(stack, name, shape, dtype):
            return stack.enter_context(nc.sbuf_tensor("sb_" + name, shape, dtype))

        def pst(stack, name, shape, dtype):
            return stack.enter_context(nc.psum_tensor("ps_" + name, shape, dtype))

        cst = sbt(es, "cst", [128, NCST], BF16)
        ident = cst[:, 0:128]
        bones = cst[:, 128:256]
        twos = cst[:, 256:320]

        def mask(i):
            return cst[:, 320 + 512 * i: 320 + 512 * (i + 1)]

        small = sbt(es, "small", [128, 32], F32)
        cwh = sbt(es, "cwh", [128, 24], F32)
        wqk = sbt(es, "wqk", [128, 1], F32)
        negc = sbt(es, "negc", [128, 1], F32)
        epsc = sbt(es, "epsc", [128, 1], F32)
        scb = sbt(es, "scb", [128, 8], BF16)
        onesf = sbt(es, "onesf", [1, 128], F32)
        grow = sbt(es, "grow", [1, 1024], F32)
        acsh = sbt(es, "acsh", [128, 16], F32)
        gate = sbt(es, "gate", [128, 1024], F32)
        hT = sbt(es, "hT", [128, 8, TOK], BF16)
        bufA = sbt(es, "bufA", [128, 8, TOK], BF16)
        hh2 = sbt(es, "hh2", [128, 8, 2], BF16)
        yb = sbt(es, "yb", [128, 4, TOK], BF16)
        NW = 10
        wring = [sbt(es, "wr%d" % i, [128, 8, 128], BF16) for i in range(NW)]
        wtok = [None] * NW
        wstate = {"i": 0}

        wpre = {}

        def wload(src, nk=8, key=None, prefetch=False):
            if key is not None and key in wpre:
                i = wpre.pop(key)
                wtok[i] = S.tok("w%d" % i)
                return wring[i], wtok[i]
            i = wstate["i"]
            wstate["i"] = (i + 1) % NW
            if wtok[i] is None or wtok[i] not in S.toks:
                wtok[i] = S.tok("w%d" % i)
            slot = wring[i]
            S.add("pool", lambda e: e.dma_start(out=slot[:, 0:nk, :], in_=src), writes=[wtok[i]], dma=True)
            if prefetch:
                wpre[key] = i
            return slot, wtok[i]

        def unit(ps_ap, ps_tok, lhs_fn, rhs_fn, nk, reads):
            for kt in range(nk):
                la, ra = lhs_fn(kt), rhs_fn(kt)
                S.add("pe", lambda e, kt=kt, la=la, ra=ra: e.matmul(ps_ap, lhsT=la, rhs=ra,
                                                                    start=(kt == 0), stop=(kt == nk - 1)),
                      reads=reads, writes=[ps_tok])

        with ExitStack() as p0:
            ccol = sbt(p0, "ccol", [128, 8], F32)
            th8 = sbt(p0, "th8", [128, 8], F32)
            modrow = sbt(p0, "modrow", [1, 3072], F32)
            nwrow = sbt(p0, "nwrow", [1, 1024], F32)
            arow = sbt(p0, "arow", [1, 1024], F32)
            qkrow = sbt(p0, "qkrow", [1, 128], F32)
            prow = sbt(p0, "prow", [1, 64], F32)
            c11 = sbt(p0, "c11", [1, 2], F32)
            xs = [sbt(p0, "xs%d" % i, [128, 1024], F32) for i in range(4)]
            xh = [sbt(p0, "xh%d" % i, [128, 1024], BF16) for i in range(8)]
            junk = sbt(p0, "junk", [128, 1024], BF16)
            ss = sbt(p0, "ss", [128, 32], F32)
            vv = sbt(p0, "vv", [128, 32], F32)
            lv = sbt(p0, "lv", [128, 32], F32)
            rstd = sbt(p0, "rstd", [128, 32], F32)
            ps_row = pst(p0, "ps_row", [128, 512], F32)
            ps_col = pst(p0, "ps_col", [128, 512], F32)
            psT = [pst(p0, "psT%d" % i, [128, 1024], BF16) for i in range(4)]

            t_cst, t_small, t_ccol, t_modrow, t_nw, t_qk = (S.tok() for _ in range(6))
            t_cst.disjoint = True
            S.add("pool", lambda e: e.dma_start(out=cst[:, 0:320], in_=cst_d[:, 0:320]), writes=[t_cst], dma=True)
            S.add("sp", lambda e: e.dma_start(out=small[:], in_=small_d[:]), writes=[t_small], dma=True)
            S.add("sp", lambda e: e.dma_start(out=ccol[:], in_=ccol_d[:]), writes=[t_ccol], dma=True)
            S.add("sp", lambda e: e.dma_start(out=modrow[:], in_=rows_d[:, 0:3072]), writes=[t_modrow], dma=True)
            S.add("sp", lambda e: e.dma_start(out=nwrow[:], in_=rows_d[:, 3072:4096]), writes=[t_nw], dma=True)
            S.add("sp", lambda e: e.dma_start(out=qkrow[:], in_=rows_d[:, 4096:4224]), writes=[t_qk], dma=True)
            t_ones = S.tok()
            S.add("dve", lambda e: e.memset(onesf[:], 1.0), writes=[t_ones])
            S.add("dve", lambda e: e.memset(epsc[:], EPS))

            t_th8, t_scb = S.tok(), S.tok()
            S.add("act", lambda e: e.activation(out=th8[:], in_=ccol[:], func=AF.Tanh, scale=0.5),
                  reads=[t_ccol], writes=[t_th8])
            S.add("dve", lambda e: e.tensor_scalar(out=th8[:], in0=th8[:], scalar1=0.5, scalar2=0.5,
                                                   op0=ALU.mult, op1=ALU.add), reads=[t_th8], writes=[t_th8])
            S.add("dve", lambda e: e.tensor_tensor(out=scb[:], in0=th8[:], in1=ccol[:], op=ALU.mult),
                  reads=[t_th8, t_ccol], writes=[t_scb])

            t_cwh, t_wqk = S.tok(), S.tok()
            S.add("dve", lambda e: e.tensor_scalar(out=cwh[:], in0=small[:, 0:24], scalar1=0.5, scalar2=None,
                                                   op0=ALU.mult), reads=[t_small], writes=[t_cwh])
            S.add("dve", lambda e: e.scalar_tensor_tensor(out=wqk[:], in0=small[:, 24:25], scalar=0.125,
                                                          in1=small[:, 25:26], op0=ALU.mult, op1=ALU.mult),
                  reads=[t_small], writes=[t_wqk])
            t_prow, t_c11 = S.tok(), S.tok()
            S.add("dve", lambda e: e.tensor_tensor(out=prow[:], in0=qkrow[:, 0:64], in1=qkrow[:, 64:128], op=ALU.mult),
                  reads=[t_qk], writes=[t_prow])
            S.add("dve", lambda e: e.reduce_max(out=c11[:, 0:1], in_=prow[:], axis=AX.X, apply_absolute_value=True),
                  reads=[t_prow], writes=[t_c11])
            S.add("dve", lambda e: e.tensor_scalar(out=c11[:, 1:2], in0=c11[:, 0:1], scalar1=-8.0, scalar2=None,
                                                   op0=ALU.mult), reads=[t_c11], writes=[t_c11])

            t_xs = [S.tok() for _ in range(4)]
            t_xh = [S.tok() for _ in range(8)]
            t_junk = S.tok(disjoint=True)
            t_psT = [S.tok(psum=True) for _ in range(4)]
            t_h3 = S.tok(disjoint=True)

            def front(gi):
                for t in range(4):
                    i = 4 * gi + t
                    xb = xs[t]
                    hb = xh[4 * (gi % 2) + t]
                    t_stat = S.tok()
                    S.add("sp", lambda e, i=i, xb=xb: e.dma_start(out=xb[:], in_=x_d[i]), writes=[t_xs[t]], dma=True)
                    S.add("act", lambda e, i=i, xb=xb: e.activation(out=junk[:], in_=xb[:], func=AF.Square,
                                                                    accum_out=ss[:, i:i + 1]),
                          reads=[t_xs[t]], writes=[t_junk, t_stat])
                    S.add("dve", lambda e, i=i: e.tensor_scalar(out=vv[:, i:i + 1], in0=ss[:, i:i + 1], scalar1=1.0 / 1024,
                                                                scalar2=EPS, op0=ALU.mult, op1=ALU.add),
                          reads=[t_stat], writes=[t_stat])
                    S.add("act", lambda e, i=i: e.activation(out=lv[:, i:i + 1], in_=vv[:, i:i + 1], func=AF.Ln),
                          reads=[t_stat], writes=[t_stat])
                    S.add("act", lambda e, i=i: e.activation(out=rstd[:, i:i + 1], in_=lv[:, i:i + 1], func=AF.Exp, scale=-0.5),
                          reads=[t_stat], writes=[t_stat])
                    S.add("dve", lambda e, i=i, xb=xb, hb=hb: e.tensor_scalar(out=hb[:], in0=xb[:], scalar1=rstd[:, i:i + 1],
                                                                              scalar2=None, op0=ALU.mult),
                          reads=[t_xs[t], t_stat], writes=[t_xh[4 * (gi % 2) + t]])

            def back(gi):
                dstbuf = bufA if gi < 4 else hT
                c0 = (gi % 4) * 512
                wr = [t_h3] if gi == 3 else []
                for j in range(4):
                    for kt in (2 * j, 2 * j + 1):
                        for t in range(4):
                            hi = 4 * (gi % 2) + t
                            S.add("pe", lambda e, j=j, kt=kt, t=t, hi=hi: e.transpose(
                                psT[j][:, (kt % 2) * 512 + t * 128:(kt % 2) * 512 + (t + 1) * 128],
                                xh[hi][:, kt * 128:(kt + 1) * 128], ident),
                                reads=[t_xh[hi], t_cst], writes=[t_psT[j]])
                    for kt in (2 * j, 2 * j + 1):
                        dst = dstbuf[:, kt, c0:c0 + 512]
                        src = psT[j][:, (kt % 2) * 512:(kt % 2) * 512 + 512]
                        if j % 2 == 0:
                            S.add("dve", lambda e, dst=dst, src=src, kt=kt: e.tensor_scalar(
                                out=dst, in0=src, scalar1=acsh[:, kt:kt + 1], scalar2=acsh[:, 8 + kt:9 + kt],
                                op0=ALU.mult, op1=ALU.add), reads=[t_psT[j], t_acsh], writes=wr)
                        else:
                            S.add("act", lambda e, dst=dst, src=src, kt=kt: e.activation(
                                out=dst, in_=src, func=AF.Identity, scale=acsh[:, kt:kt + 1], bias=acsh[:, 8 + kt:9 + kt]),
                                reads=[t_psT[j], t_acsh], writes=wr)
                if gi == 3:
                    S.add("dve", lambda e: e.tensor_copy(out=hh2[:], in_=bufA[:, :, TOK - 2:TOK]), reads=[t_h3])

            t_acsh = S.tok()
            front(0)
            front(1)

            t_psrow = S.tok(psum=True)

            def mod_chunk(ci):
                for j in range(4):
                    ct = 4 * ci + j
                    slot, wt = wload(wada_d[ct])
                    unit(ps_row[0:1, j * 128:(j + 1) * 128], t_psrow,
                         lambda kt: scb[:, kt:kt + 1], lambda kt, slot=slot: slot[:, kt, :], 8, [wt, t_scb])
                c0 = ci * 512
                S.add("dve", lambda e, c0=c0: e.tensor_tensor(out=modrow[:, c0:c0 + 512], in0=modrow[:, c0:c0 + 512],
                                                              in1=ps_row[0:1, :], op=ALU.add),
                      reads=[t_psrow, t_modrow], writes=[t_modrow])

            for ci in (2, 3, 0, 1):
                mod_chunk(ci)
            t_arow = S.tok()
            S.add("dve", lambda e: e.scalar_tensor_tensor(out=arow[:], in0=modrow[:, 1024:2048], scalar=1.0, in1=nwrow[:],
                                                          op0=ALU.add, op1=ALU.mult),
                  reads=[t_modrow, t_nw], writes=[t_arow])
            t_pscol, t_negc = S.tok(psum=True), S.tok()
            for kt in range(8):
                S.add("pe", lambda e, kt=kt: e.matmul(ps_col[:, kt:kt + 1], lhsT=arow[0:1, kt * 128:(kt + 1) * 128],
                                                      rhs=onesf[0:1, 0:1], start=True, stop=True),
                      reads=[t_arow, t_ones], writes=[t_pscol])
                S.add("pe", lambda e, kt=kt: e.matmul(ps_col[:, 8 + kt:9 + kt], lhsT=modrow[0:1, kt * 128:(kt + 1) * 128],
                                                      rhs=onesf[0:1, 0:1], start=True, stop=True),
                      reads=[t_modrow, t_ones], writes=[t_pscol])
            S.add("pe", lambda e: e.matmul(ps_col[:, 16:17], lhsT=onesf[0:1, 0:128], rhs=c11[0:1, 1:2],
                                           start=True, stop=True), reads=[t_c11, t_ones], writes=[t_pscol])
            S.add("dve", lambda e: e.tensor_copy(out=acsh[:], in_=ps_col[:, 0:16]), reads=[t_pscol], writes=[t_acsh])
            S.add("dve", lambda e: e.tensor_copy(out=negc[:], in_=ps_col[:, 16:17]), reads=[t_pscol], writes=[t_negc])

            for gi in range(8):
                back(gi)
                if gi + 2 < 8:
                    front(gi + 2)

            S.add("dve", lambda e: e.tensor_copy(out=grow[:], in_=modrow[:, 2048:3072]), reads=[t_modrow])
            for ti in (44, 32, 56, 48):
                wload(win_d[ti], key=("win", ti), prefetch=True)
            S.emit()
        def dump0():
            for dst, src in ((d_hT, hT), (d_hh, bufA), (d_acsh, acsh), (d_gate, gate), (d_negc, negc)):
                S.add("sp", lambda e, dst=dst, src=src: e.dma_start(out=dst[:], in_=src[:]), dma=True)

        if STOP <= 0:
            if DEBUG:
                dump0()
                S.emit()
            return nc

        HG = (128, 512, 2048)
        DIL = (1, 4, 16)
        with ExitStack() as p1:
            if DEBUG:
                dump0()
            Qn = [sbt(p1, "Qn%d" % g, [128, TOK], BF16) for g in range(3)]
            print("p1 sbuf remaining before rest", nc.sbuf_bytes_remaining)
            Kn = [sbt(p1, "Kn%d" % g, [128, HG[g] + TOK], BF16) for g in range(3)]
            Vt = [sbt(p1, "Vt%d" % g, [128, HG[g] // 128 + 16, 128], BF16) for g in range(3)]
            sq = [sbt(p1, "sq%d" % i, [128, 512], BF16) for i in range(2)]
            lnv = [sbt(p1, "lnv%d" % i, [128, 512], F32) for i in range(2)]
            NE = 6
            Eb = [sbt(p1, "Eb%d" % i, [128, 512], BF16) for i in range(NE)]
            Pb = [sbt(p1, "Pb%d" % i, [128, 512], BF16) for i in range(NE)]
            zs = sbt(p1, "zs", [128, TOK], BF16)
            thz = [sbt(p1, "thz%d" % i, [128, 512], F32) for i in range(2)]
            rec = [sbt(p1, "rec%d" % i, [128, 512], F32) for i in range(2)]
            ton = [sbt(p1, "ton%d" % i, [128, 512], F32) for i in range(2)]
            bk = [pst(p1, "bk%d" % i, [128, 512], F32) for i in range(8)]
            tbk = [S.tok(psum=True) for _ in range(8)]
            NPP = 5
            pp, t_pp = bk[0:5], tbk[0:5]
            pq, t_pq = bk[5:7], tbk[5:7]
            NS = 4
            pS, t_pS = bk[0:4], tbk[0:4]
            pOs, t_pOs = [bk[4], bk[6]], [tbk[4], tbk[6]]
            pMs, t_pMs = [bk[5], bk[7]], [tbk[5], tbk[7]]

            t_sq = [S.tok(), S.tok()]
            t_lnv = [S.tok(), S.tok()]
            t_Q = [S.tok(disjoint=True) for _ in range(3)]
            t_K = [S.tok(disjoint=True) for _ in range(3)]
            t_V = [S.tok(disjoint=True) for _ in range(3)]
            t_E = [S.tok() for _ in range(NE)]
            t_P = [S.tok() for _ in range(NE)]
            t_zs = S.tok(disjoint=True)
            t_thz = [S.tok(), S.tok()]
            t_rec = [S.tok(), S.tok()]
            t_ton = [S.tok(), S.tok()]
            t_yb = S.tok(disjoint=True)
            ucnt = {"u": 0}

            def qk_unit(slot, wt, src_fn, N, dst_ap, dst_tok, scalar, r, dst_ap2=None):
                u = ucnt["u"]
                ucnt["u"] += 1
                pb = u % NPP
                qc = ucnt.get("q", 0)
                ucnt["q"] = qc + 1
                b = qc % 2
                ps = pp[pb][:, 0:N]
                unit(ps, t_pp[pb], lambda kt: slot[:, kt, :], src_fn, 8, [wt])
                S.add("act", lambda e: e.activation(out=sq[b][:, 0:N], in_=ps, func=AF.Square),
                      reads=[t_pp[pb]], writes=[t_sq[b]])
                return lambda: qk_part2(ps, pb, b, N, dst_ap, dst_tok, scalar, r, dst_ap2)

            def qk_part2(ps, pb, b, N, dst_ap, dst_tok, scalar, r, dst_ap2):
                S.add("pe", lambda e: e.matmul(pq[b][:, 0:N], lhsT=bones, rhs=sq[b][:, 0:N], start=True, stop=True),
                      reads=[t_sq[b]], writes=[t_pq[b]])
                S.add("act", lambda e: e.activation(out=lnv[b][:, 0:N], in_=pq[b][:, 0:N], func=AF.Ln, bias=EPS),
                      reads=[t_pq[b]], writes=[t_lnv[b]])
                S.add("act", lambda e: e.activation(out=lnv[b][:, 0:N], in_=lnv[b][:, 0:N], func=AF.Exp, scale=-0.5),
                      reads=[t_lnv[b]], writes=[t_lnv[b]])
                if dst_ap2 is None:
                    S.add("dve", lambda e: e.scalar_tensor_tensor(out=dst_ap, in0=ps_view(ps, r), scalar=scalar,
                                                                  in1=ps_view(lnv[b][:, 0:N], r),
                                                                  op0=ALU.mult, op1=ALU.mult),
                          reads=[t_pp[pb], t_lnv[b]], writes=[dst_tok])
                else:
                    for hd, d_ap in enumerate((dst_ap, dst_ap2)):
                        rows = slice(64 * hd, 64 * hd + 64)
                        S.add("dve", lambda e, rows=rows, d_ap=d_ap: e.scalar_tensor_tensor(
                            out=d_ap, in0=ps_view(pp[pb][rows, 0:N], r), scalar=scalar[rows, :],
                            in1=ps_view(lnv[b][rows, 0:N], r), op0=ALU.mult, op1=ALU.mult),
                            reads=[t_pp[pb], t_lnv[b]], writes=[dst_tok])

            view_state = {}

            def ps_view(src, r):
                if r == 1:
                    return src
                return src.rearrange("p (a r) -> p r a", r=r)

            def dst_view(buf, base, Lsub, r, a0, na):
                if r == 1:
                    return buf[:, base + a0: base + a0 + na]
                return buf[:, base:base + r * Lsub].rearrange("p (r l) -> p r l", r=r)[:, :, a0:a0 + na]

            def load_pair(m):
                W = {}
                for g in range(3):
                    W[("k", g)] = wload(win_d[44 + 4 * g + m], key=("win", 44 + 4 * g + m))
                    W[("q", g)] = wload(win_d[32 + 4 * g + m], key=("win", 32 + 4 * g + m))
                    W[("v", g)] = wload(win_d[56 + 4 * g + m], key=("win", 56 + 4 * g + m))
                W["z"] = wload(win_d[68 + m])
                return W

            Wnext = load_pair(0)
            t_msk = S.tok(disjoint=True)
            for c0 in range(320, NCST, 1152):
                S.add("pool", lambda e, c0=c0: e.dma_start(out=cst[:, c0:c0 + 1152], in_=cst_d[:, c0:c0 + 1152]),
                      writes=[t_msk], dma=True)
            for m in range(4):
                W = Wnext
                qk_list, v_list, z_list = [], [], []
                for g in range(3):
                    r = DIL[g]
                    H = HG[g]
                    L = TOK // r
                    slotk, wtk = W[("k", g)]
                    nh = max(1, H // 512)
                    for u in range(nh):
                        N = min(512, H)
                        c0 = TOK - H + 512 * u
                        na = N // r
                        dst = dst_view(Kn[g], 0, 128, r, (512 * u) // r, na)
                        qk_list.append(lambda slotk=slotk, wtk=wtk, c0=c0, N=N, dst=dst, g=g, r=r: qk_unit(
                            slotk, wtk, lambda kt: bufA[:, kt, c0:c0 + N], N, dst, t_K[g], 1.0, r))
                    for n in range(4):
                        dst = dst_view(Kn[g], H, L, r, (512 * n) // r, 512 // r)
                        qk_list.append(lambda slotk=slotk, wtk=wtk, n=n, dst=dst, g=g, r=r: qk_unit(
                            slotk, wtk, lambda kt: hT[:, kt, 512 * n:512 * n + 512], 512, dst, t_K[g], 1.0, r))
                    slotq, wtq = W[("q", g)]
                    for n in range(4):
                        dstq = dst_view(Qn[g], 0, L, r, (512 * n) // r, 512 // r)
                        qk_list.append(lambda slotq=slotq, wtq=wtq, n=n, dstq=dstq, g=g, r=r: qk_unit(
                            slotq, wtq, lambda kt: hT[:, kt, 512 * n:512 * n + 512], 512, dstq, t_Q[g], wqk[:, 0:1], r))
                    slotv, wtv = W[("v", g)]
                    nhb = H // 128
                    nkb = nhb + 16

                    def v_group(kb0, g=g, r=r, H=H, L=L, slotv=slotv, wtv=wtv, nhb=nhb, nkb=nkb):
                        u = ucnt["u"]
                        ucnt["u"] += 1
                        b = u % NPP
                        nb = min(4, nkb - kb0)
                        for j in range(nb):
                            kb = kb0 + j
                            if kb < nhb:
                                srcbuf, start = bufA, TOK - H + kb
                            else:
                                o = kb - nhb
                                rr, bb = o // (L // 128), o % (L // 128)
                                srcbuf, start = hT, 128 * bb * r + rr
                            if r == 1:
                                lhs_fn = lambda kt, srcbuf=srcbuf, start=start: srcbuf[:, kt, start:start + 128]
                            else:
                                lhs_fn = lambda kt, srcbuf=srcbuf, start=start, r=r: \
                                    srcbuf[:, kt, start - (start % r):start - (start % r) + 128 * r].rearrange(
                                        "p (a r) -> p r a", r=r)[:, start % r, :]
                            unit(pp[b][:, j * 128:(j + 1) * 128], t_pp[b], lhs_fn,
                                 lambda kt: slotv[:, kt, :], 8, [wtv])
                        S.add("dve", lambda e, b=b, nb=nb, g=g, kb0=kb0: e.tensor_copy(
                            out=Vt[g][:, kb0:kb0 + nb, :].rearrange("p a b -> p (a b)"), in_=pp[b][:, 0:nb * 128]),
                            reads=[t_pp[b]], writes=[t_V[g]])

                    for kb0 in range(0, nkb, 4):
                        v_list.append(lambda kb0=kb0, v_group=v_group: v_group(kb0))
                slotz, wtz = W["z"]

                def z_unit(n, slotz=slotz, wtz=wtz):
                    u = ucnt["u"]
                    ucnt["u"] += 1
                    b = u % NPP
                    zb = n % 2
                    unit(pp[b][:, :], t_pp[b], lambda kt: slotz[:, kt, :],
                         lambda kt: hT[:, kt, 512 * n:512 * n + 512], 8, [wtz])
                    S.add("act", lambda e, b=b, zb=zb: e.activation(out=thz[zb][:], in_=pp[b][:, :], func=AF.Tanh, scale=0.5),
                          reads=[t_pp[b]], writes=[t_thz[zb]])
                    S.add("dve", lambda e, b=b, n=n, zb=zb: e.scalar_tensor_tensor(
                        out=zs[:, 512 * n:512 * n + 512], in0=thz[zb][:], scalar=1.0, in1=pp[b][:, :],
                        op0=ALU.add, op1=ALU.mult), reads=[t_thz[zb], t_pp[b]], writes=[t_zs])

                for n in range(4):
                    z_list.append(lambda n=n, z_unit=z_unit: z_unit(n))
                light = v_list
                pend = None
                while qk_list or light:
                    cont = qk_list.pop(0)() if qk_list else None
                    if light:
                        light.pop(0)()
                        if pend is not None:
                            pend()
                            pend = None
                        if cont is not None:
                            cont()
                    else:
                        if pend is not None:
                            pend()
                        pend = cont
                if pend is not None:
                    pend()
                for zf in z_list:
                    zf()
                if m + 1 < 4:
                    Wnext = load_pair(m + 1)
                else:
                    for ti in (8, 16, 24, 0):
                        wload(win_d[ti], key=("win", ti), prefetch=True)

                def batches(n):
                    res = []
                    i0 = 4 * n
                    if n == 0:
                        prev = (0, 0, i0 * 128, 128, (0, 1))
                    else:
                        prev = (0, 128 + (i0 - 1) * 128, i0 * 128, 128, (0, 1))
                    last = (0, 128 + (i0 + 3) * 128, (i0 + 3) * 128, 128, (384, 1))
                    res.append((M_T0H if n == 0 else M_T0, [prev, last, (0, 128 + i0 * 128, i0 * 128, 256, (0, 1))]))
                    res.append((M_T1, [(0, 128 + (i0 + j) * 128, (i0 + j) * 128, 256, (j * 128, 1)) for j in (1, 2)]))
                    for rp in range(2):
                        pcs = []
                        for rr in (2 * rp, 2 * rp + 1):
                            if n == 0:
                                pcs.append((1, rr * 128, rr * 512, 128, (rr, 4)))
                            else:
                                pcs.append((1, 512 + rr * 512 + (n - 1) * 128, rr * 512 + n * 128, 128, (rr, 4)))
                            pcs.append((1, 512 + rr * 512 + n * 128, rr * 512 + n * 128, 128, (rr, 4)))
                        res.append((M_T2H if n == 0 else M_T2, pcs))
                    for rb in range(2):
                        pcs = []
                        for rr in range(8 * rb, 8 * rb + 8):
                            pcs.append((2, rr * 128, rr * 128 + 32 * n, 32, (rr, 16)))
                            pcs.append((2, 2048 + rr * 128, rr * 128 + 32 * n, 32, (rr, 16)))
                        res.append((M_T3 + n, pcs))
                    return res

                blist = []
                for n in range(4):
                    bl = batches(n)
                    for bi, (mi, pcs) in enumerate(bl):
                        blist.append((n, bi == 0, bi == len(bl) - 1, mi, pcs))
                NB = len(blist)

                def emit_S(bidx):
                    n, first, lastb, mi, pcs = blist[bidx]
                    off = 0
                    for (g, kcol, qcol, N, ospec) in pcs:
                        for hd in range(2):
                            sb3 = (2 * bidx + hd) % NS
                            rows = slice(64 * hd, 64 * hd + 64)
                            S.add("pe", lambda e, g=g, kcol=kcol, qcol=qcol, N=N, rows=rows, off=off, sb3=sb3: e.matmul(
                                pS[sb3][:, off:off + N], lhsT=Kn[g][rows, kcol:kcol + 128],
                                rhs=Qn[g][rows, qcol:qcol + N], start=True, stop=True),
                                reads=[t_K[g], t_Q[g]], writes=[t_pS[sb3]])
                        off += N
                    assert off == 512
                    for hd in range(2):
                        sb3 = (2 * bidx + hd) % NS
                        eb = (2 * bidx + hd) % NE
                        S.add("act", lambda e, sb3=sb3, eb=eb: e.activation(out=Eb[eb][:], in_=pS[sb3][:, :], func=AF.Exp,
                                                                            bias=negc[:, 0:1], scale=1.0),
                              reads=[t_pS[sb3]], writes=[t_E[eb]])
                        S.add("dve", lambda e, eb=eb, mi=mi: e.tensor_tensor(out=Pb[eb][:], in0=Eb[eb][:], in1=mask(mi),
                                                                             op=ALU.mult),
                              reads=[t_E[eb], t_msk], writes=[t_P[eb]])

                def emit_PV(bidx):
                    n, first, lastb, mi, pcs = blist[bidx]
                    pO, t_pO, pM, t_pM = pOs[n % 2], t_pOs[n % 2], pMs[n % 2], t_pMs[n % 2]
                    np_ = len(pcs)
                    for hd in range(2):
                        eb = (2 * bidx + hd) % NE
                        off = 0
                        for pi, (g, kcol, qcol, N, (ostart, ostep)) in enumerate(pcs):
                            vkb = kcol // 128
                            st = first and pi == 0
                            sp_ = lastb and pi == np_ - 1
                            rows = slice(64 * hd, 64 * hd + 64)

                            def oview(t, rows=rows, ostart=ostart, ostep=ostep, N=N):
                                if ostep == 1:
                                    return t[rows, ostart:ostart + N]
                                return t[rows, :].rearrange("p (a r) -> p r a", r=ostep)[:, ostart, 0:N]

                            S.add("pe", lambda e, g=g, vkb=vkb, hd=hd, off=off, N=N, eb=eb, st=st, sp_=sp_, oview=oview:
                                  e.matmul(oview(pO), lhsT=Vt[g][:, vkb, 64 * hd:64 * hd + 64],
                                           rhs=Pb[eb][:, off:off + N], start=st, stop=sp_, skip_group_check=True),
                                  reads=[t_P[eb], t_V[g]], writes=[t_pO])
                            S.add("pe", lambda e, hd=hd, off=off, N=N, eb=eb, st=st, sp_=sp_, oview=oview:
                                  e.matmul(oview(pM), lhsT=twos, rhs=Pb[eb][:, off:off + N], start=st, stop=sp_,
                                           skip_group_check=True),
                                  reads=[t_P[eb]], writes=[t_pM])
                            off += N
                    if lastb:
                        b = n % 2
                        S.add("act", lambda e, b=b, pM=pM: e.activation(out=rec[b][:], in_=pM[:, :], func=AF.Ln),
                              reads=[t_pM], writes=[t_rec[b]])
                        S.add("act", lambda e, b=b: e.activation(out=rec[b][:], in_=rec[b][:], func=AF.Exp, scale=-1.0),
                              reads=[t_rec[b]], writes=[t_rec[b]])
                        S.add("dve", lambda e, b=b, pO=pO: e.tensor_tensor(out=ton[b][:], in0=pO[:, :], in1=rec[b][:], op=ALU.mult),
                              reads=[t_pO, t_rec[b]], writes=[t_ton[b]])
                        S.add("pool", lambda e, b=b, n=n, m=m: e.tensor_tensor(
                            out=yb[:, m, 512 * n:512 * n + 512], in0=ton[b][:], in1=zs[:, 512 * n:512 * n + 512],
                            op=ALU.mult), reads=[t_ton[b], t_zs], writes=[t_yb])

                emit_S(0)
                emit_S(1)
                for bidx in range(NB):
                    if bidx + 2 < NB:
                        emit_S(bidx + 2)
                    emit_PV(bidx)
            if DEBUG and STOP <= 1:
                S.emit()
                for g in range(3):
                    for hd in range(2):
                        S.add("sp", lambda e, g=g, hd=hd: e.dma_start(out=d_Q[g][hd][:], in_=Qn[g][hd][:]), dma=True)
                    S.add("sp", lambda e, g=g: e.dma_start(out=d_K[g][:], in_=Kn[g][:]), dma=True)
                    S.add("sp", lambda e, g=g: e.dma_start(out=d_V[g][:], in_=Vt[g][:]), dma=True)
                S.add("sp", lambda e: e.dma_start(out=d_zs[:], in_=zs[:]), dma=True)
            S.emit()
        if STOP <= 1:
            if DEBUG:
                S.add("sp", lambda e: e.dma_start(out=d_yb[:], in_=yb[:]), dma=True)
                S.emit()
            return nc

        ya = bufA
        with ExitStack() as p2:
            if DEBUG:
                S.add("sp", lambda e: e.dma_start(out=d_yb[:], in_=yb[:]), dma=True)
            csb = [sbt(p2, "csb%d" % i, [128, 512], F32) for i in range(2)]
            ub = [sbt(p2, "ub%d" % i, [128, 514], F32) for i in range(2)]
            tb = [sbt(p2, "tb%d" % i, [128, 512], F32) for i in range(2)]
            thb = [sbt(p2, "thb%d" % i, [128, 512], F32) for i in range(2)]
            szb = [sbt(p2, "szb%d" % i, [128, 512], F32) for i in range(2)]
            bzb = [sbt(p2, "bzb%d" % i, [128, 512], F32) for i in range(2)]
            ch2 = sbt(p2, "ch2", [128, 2], F32)
            pc = [pst(p2, "pc%d" % i, [128, 512], F32) for i in range(6)]
            ph = pst(p2, "ph", [128, 512], F32)
            t_pc = [S.tok(psum=True) for _ in range(6)]
            t_ph = S.tok(psum=True)
            t_csb = [S.tok(), S.tok()]
            t_ub = [S.tok(), S.tok()]
            t_tb = [S.tok(), S.tok()]
            t_thb = [S.tok(), S.tok()]
            t_szb = [S.tok(), S.tok()]
            t_bzb = [S.tok(), S.tok()]
            t_ch2 = S.tok()
            t_ya = S.tok(disjoint=True)
            pcn = {"i": 0}

            def nextp():
                i = pcn["i"]
                pcn["i"] = (i + 1) % 6
                return pc[i], t_pc[i]

            it = 0

            def load_ct(ct):
                return [wload(win_d[t], key=("win", t)) for t in (8 + ct, 16 + ct, 24 + ct, ct)]

            gs = [sbt(p2, "gs%d" % i, [128, 8, 128], BF16) for i in range(4)]
            t_gs = [S.tok() for _ in range(4)]
            pg = pst(p2, "pg", [128, 512], F32)
            t_pg = S.tok(psum=True)
            t_grow, t_gate = S.tok(), S.tok(disjoint=True)

            accg = [sbt(p2, "accg%d" % i, [128, 128], F32) for i in range(2)]
            onesm = sbt(p2, "onesm", [128, 128], F32)
            t_accg = [S.tok(), S.tok()]
            t_onesm = S.tok()
            S.add("dve", lambda e: e.memset(onesm[:], 1.0), writes=[t_onesm])

            def gate_tile(ct):
                b = ct % 2
                g = gs[ct % 4]
                S.add("dve", lambda e: e.tensor_scalar(out=accg[b][:], in0=g[:, 0, :], scalar1=scb[:, 0:1], scalar2=None,
                                                       op0=ALU.mult), reads=[t_gs[ct % 4]], writes=[t_accg[b]])
                for kt in range(1, 8):
                    S.add("dve", lambda e, kt=kt: e.scalar_tensor_tensor(out=accg[b][:], in0=g[:, kt, :], scalar=scb[:, kt:kt + 1],
                                                                         in1=accg[b][:], op0=ALU.mult, op1=ALU.add),
                          reads=[t_gs[ct % 4], t_accg[b]], writes=[t_accg[b]])
                S.add("dve", lambda e: e.tensor_tensor(out=accg[b][0:1, :], in0=accg[b][0:1, :],
                                                       in1=grow[0:1, ct * 128:(ct + 1) * 128], op=ALU.add),
                      reads=[t_accg[b]], writes=[t_accg[b]])
                j = ct % 4
                S.add("pe", lambda e: e.matmul(pg[:, j * 128:(j + 1) * 128], lhsT=onesm[:], rhs=accg[b][:], start=True, stop=True),
                      reads=[t_accg[b], t_onesm], writes=[t_pg])
                if j == 3:
                    h = ct // 4
                    S.add("act", lambda e: e.activation(out=gate[:, 512 * h:512 * h + 512], in_=pg[:, :],
                                                        func=AF.Copy, scale=0.5), reads=[t_pg], writes=[t_gate])

            Wn = load_ct(0)
            for ct in range(8):
                (sC, wC), (sX, wX), (sZ, wZ), (sB, wB) = Wn
                if ct + 1 < 8:
                    Wn = load_ct(ct + 1)
                if ct >= 1:
                    gate_tile(ct - 1)
                S.add("pool", lambda e, ct=ct: e.dma_start(out=gs[ct % 4][:], in_=wada_d[16 + ct]),
                      writes=[t_gs[ct % 4]], dma=True)
                if ct == 7:
                    wload(win_d[72], key=("win", 72), prefetch=True)
                    wload(wbc_d[0], key=("wbc", 0), prefetch=True)
                    wload(win_d[80], key=("win", 80), prefetch=True)
                    wload(wba_d[0], nk=4, key=("wba", 0), prefetch=True)
                unit(ph[:, 0:2], t_ph, lambda kt: sC[:, kt, :], lambda kt: hh2[:, kt, :], 8, [wC])
                unit(ph[:, 2:4], t_ph, lambda kt: sX[:, kt, :], lambda kt: hh2[:, kt, :], 8, [wX])
                S.add("act", lambda e: e.activation(out=ch2[:], in_=ph[:, 0:2], func=AF.Copy, scale=small[:, 26:27]),
                      reads=[t_ph], writes=[t_ch2])
                for n in range(4):
                    b = it % 2
                    it += 1
                    rhs_fn = lambda kt, n=n: hT[:, kt, 512 * n:512 * n + 512]
                    pC, tC = nextp()
                    unit(pC[:, :], tC, lambda kt: sC[:, kt, :], rhs_fn, 8, [wC])
                    S.add("act", lambda e, b=b, pC=pC: e.activation(out=csb[b][:], in_=pC[:, :], func=AF.Copy),
                          reads=[tC], writes=[t_csb[b]])
                    pX, tX = nextp()
                    unit(pX[:, :], tX, lambda kt: sX[:, kt, :], rhs_fn, 8, [wX])
                    if n == 0:
                        S.add("dve", lambda e, b=b: e.tensor_tensor(out=ub[b][:, 0:2], in0=ph[:, 2:4], in1=ch2[:], op=ALU.mult),
                              reads=[t_ph, t_ch2], writes=[t_ub[b]])
                    else:
                        S.add("dve", lambda e, b=b: e.tensor_copy(out=ub[b][:, 0:2], in_=ub[1 - b][:, 512:514]),
                              reads=[t_ub[1 - b]], writes=[t_ub[b]])
                    S.add("dve", lambda e, b=b, pX=pX: e.tensor_tensor(out=ub[b][:, 2:514], in0=pX[:, :], in1=csb[b][:], op=ALU.mult),
                          reads=[tX, t_csb[b], t_ub[b]], writes=[t_ub[b]])
                    S.add("dve", lambda e, b=b, ct=ct: e.tensor_scalar(out=tb[b][:], in0=ub[b][:, 2:514],
                                                                       scalar1=cwh[:, 3 * ct + 2:3 * ct + 3], scalar2=None,
                                                                       op0=ALU.mult), reads=[t_ub[b]], writes=[t_tb[b]])
                    S.add("dve", lambda e, b=b, ct=ct: e.scalar_tensor_tensor(out=tb[b][:], in0=ub[b][:, 1:513],
                                                                              scalar=cwh[:, 3 * ct + 1:3 * ct + 2], in1=tb[b][:],
                                                                              op0=ALU.mult, op1=ALU.add),
                          reads=[t_ub[b], t_tb[b]], writes=[t_tb[b]])
                    S.add("dve", lambda e, b=b, ct=ct: e.scalar_tensor_tensor(out=tb[b][:], in0=ub[b][:, 0:512],
                                                                              scalar=cwh[:, 3 * ct:3 * ct + 1], in1=tb[b][:],
                                                                              op0=ALU.mult, op1=ALU.add),
                          reads=[t_ub[b], t_tb[b]], writes=[t_tb[b]])
                    pZ, tZ = nextp()
                    unit(pZ[:, :], tZ, lambda kt: sZ[:, kt, :], rhs_fn, 8, [wZ])
                    S.add("act", lambda e, b=b, pZ=pZ: e.activation(out=thb[b][:], in_=pZ[:, :], func=AF.Tanh, scale=0.5),
                          reads=[tZ], writes=[t_thb[b]])
                    S.add("dve", lambda e, b=b, pZ=pZ: e.scalar_tensor_tensor(out=szb[b][:], in0=thb[b][:], scalar=1.0, in1=pZ[:, :],
                                                                              op0=ALU.add, op1=ALU.mult),
                          reads=[t_thb[b], tZ], writes=[t_szb[b]])
                    pB, tB = nextp()
                    unit(pB[:, :], tB, lambda kt: sB[:, kt, :], rhs_fn, 8, [wB])
                    S.add("dve", lambda e, b=b, pB=pB: e.tensor_tensor(out=bzb[b][:], in0=pB[:, :], in1=szb[b][:], op=ALU.mult),
                          reads=[tB, t_szb[b]], writes=[t_bzb[b]])
                    S.add("pool", lambda e, b=b, ct=ct, n=n: e.tensor_tensor(out=ya[:, ct, 512 * n:512 * n + 512], in0=bzb[b][:],
                                                                            in1=tb[b][:], op=ALU.mult),
                          reads=[t_bzb[b], t_tb[b]], writes=[t_ya])
            gate_tile(7)
            S.emit()
        if STOP <= 2:
            if DEBUG:
                S.add("sp", lambda e: e.dma_start(out=d_yb[:], in_=yb[:]), dma=True)
                S.add("sp", lambda e: e.dma_start(out=d_ya[:], in_=bufA[:]), dma=True)
                S.emit()
            return nc

        with ExitStack() as p34:
            mg = sbt(p34, "mg", [128, 8, TOK], BF16)
            wo = sbt(p34, "wo", [128, 8, 1024], BF16)
            xr = [sbt(p34, "xr%d" % i, [128, 1024], F32) for i in range(3)]
            tha = [sbt(p34, "tha%d" % i, [128, 512], F32) for i in range(2)]
            tgb = [sbt(p34, "tgb%d" % i, [128, 512], F32) for i in range(2)]
            tg = [sbt(p34, "tg%d" % i, [128, 1024], F32) for i in range(2)]
            ob = [sbt(p34, "ob%d" % i, [128, 1024], F32) for i in range(2)]
            pc = [pst(p34, "pd%d" % i, [128, 512], F32) for i in range(6)]
            po = [pst(p34, "po%d" % i, [128, 512], F32) for i in range(2)]
            if DEBUG:
                S.add("sp", lambda e: e.dma_start(out=d_ya[:], in_=ya[:]), dma=True)
            t_pc = [S.tok(psum=True) for _ in range(6)]
            t_po = [S.tok(psum=True) for _ in range(2)]
            t_tha = [S.tok(), S.tok()]
            t_tgb = [S.tok(), S.tok()]
            ma, t_ma, mb, t_mb = tha, t_tha, tgb, t_tgb
            t_mg = [S.tok(disjoint=True), S.tok(disjoint=True)]
            t_wo = S.tok(disjoint=True)
            t_xr = [S.tok() for _ in range(3)]
            t_tg = [S.tok(disjoint=True), S.tok(disjoint=True)]
            t_ob = [S.tok() for _ in range(2)]
            pcn = {"i": 0}

            def nextp3():
                i = pcn["i"]
                pcn["i"] = (i + 1) % 6
                return pc[i], t_pc[i]

            def load_ft(ft):
                return [wload(win_d[72 + ft], key=("win", 72 + ft)), wload(wbc_d[ft], key=("wbc", ft)),
                        wload(win_d[80 + ft], key=("win", 80 + ft)), wload(wba_d[ft], nk=4, key=("wba", ft))]

            def xload(i):
                xb = i % 3
                S.add("sp", lambda e: e.dma_start(out=xr[xb][:], in_=x_d[16 + i]), writes=[t_xr[xb]], dma=True)

            def out_tile(i):
                b = i % 2
                xb = i % 3
                hf = i // 8
                for h in range(2):
                    unit(po[h][:, :], t_po[h], lambda kt: mg[:, kt, 128 * i:128 * i + 128],
                         lambda kt: wo[:, kt, 512 * h:512 * h + 512], 8, [t_mg[hf], t_wo])
                    S.add("dve", lambda e, h=h: e.tensor_tensor(out=tg[b][:, 512 * h:512 * h + 512], in0=po[h][:, :],
                                                                in1=gate[:, 512 * h:512 * h + 512], op=ALU.mult),
                          reads=[t_po[h]], writes=[t_tg[b]])
                S.add("pool", lambda e: e.tensor_tensor(out=ob[b][:], in0=tg[b][:], in1=xr[xb][:], op=ALU.add),
                      reads=[t_tg[b], t_xr[xb]], writes=[t_ob[b]])
                S.add("sp", lambda e: e.dma_start(out=out_d[i], in_=ob[b][:]), reads=[t_ob[b]], dma=True)
                if i + 3 < 16:
                    xload(i + 3)

            it = 0
            for half in range(2):
                Wn = load_ft(0)
                for ft in range(8):
                    (sGa, wGa), (sPa, wPa), (sGb, wGb), (sPb, wPb) = Wn
                    if ft + 1 < 8:
                        Wn = load_ft(ft + 1)
                    if half == 0 and ft == 1:
                        for kt in range(8):
                            S.add("pool", lambda e, kt=kt: e.dma_start(out=wo[:, kt, :], in_=wout_d[kt * 128:(kt + 1) * 128, :]),
                                  writes=[t_wo], dma=True)
                        for i in range(3):
                            xload(i)
                    for n in (2 * half, 2 * half + 1):
                        b = it % 2
                        it += 1
                        cs = slice(512 * n, 512 * n + 512)
                        pGa, tGa = nextp3()
                        unit(pGa[:, :], tGa, lambda kt: sGa[:, kt, :], lambda kt: hT[:, kt, cs], 8, [wGa])
                        S.add("act", lambda e, b=b, pGa=pGa: e.activation(out=tha[b][:], in_=pGa[:, :], func=AF.Tanh, scale=0.5),
                              reads=[tGa], writes=[t_tha[b]])
                        pA, tA = nextp3()
                        unit(pA[:, :], tA, lambda kt: sPa[:, kt, :], lambda kt: ya[:, kt, cs], 8, [wPa])
                        S.add("dve", lambda e, b=b, pA=pA: e.scalar_tensor_tensor(out=ma[b][:], in0=tha[b][:], scalar=1.0, in1=pA[:, :],
                                                                                  op0=ALU.add, op1=ALU.mult),
                              reads=[t_tha[b], tA], writes=[t_ma[b]])
                        pGb, tGb = nextp3()
                        unit(pGb[:, :], tGb, lambda kt: sGb[:, kt, :], lambda kt: hT[:, kt, cs], 8, [wGb])
                        S.add("act", lambda e, b=b, pGb=pGb: e.activation(out=tgb[b][:], in_=pGb[:, :], func=AF.Tanh, scale=0.5),
                              reads=[tGb], writes=[t_tgb[b]])
                        pBm, tBm = nextp3()
                        unit(pBm[:, :], tBm, lambda kt: sPb[:, kt, :], lambda kt: yb[:, kt, cs], 4, [wPb])
                        S.add("dve", lambda e, b=b, pBm=pBm: e.scalar_tensor_tensor(out=mb[b][:], in0=tgb[b][:], scalar=1.0, in1=pBm[:, :],
                                                                                    op0=ALU.add, op1=ALU.mult),
                              reads=[t_tgb[b], tBm], writes=[t_mb[b]])
                        S.add("pool", lambda e, b=b, ft=ft, cs=cs: e.tensor_tensor(out=mg[:, ft, cs], in0=ma[b][:], in1=mb[b][:], op=ALU.add),
                              reads=[t_ma[b], t_mb[b]], writes=[t_mg[half]])
                    if half == 1:
                        out_tile(ft)
            for i in range(8, 16):
                out_tile(i)
            S.emit(final=True)
    return nc


def _consts(core):
    q = core % 4
    hv = 1.0 if q > 0 else 0.0
    k = np.arange(128)[:, None]
    qq = np.arange(128)[None, :]
    D = (k <= qq).astype(np.float32)
    U = (k >= qq).astype(np.float32)
    hU = U * hv
    cst = np.zeros((128, NCST), np.float32)
    cst[:, 0:128] = np.eye(128, dtype=np.float32)
    bo = np.zeros((128, 128), np.float32)
    bo[0:64, 0:64] = 1.0 / 64
    bo[64:128, 64:128] = 1.0 / 64
    cst[:, 128:256] = bo
    cst[:, 256:320] = 2.0
    tiles = [np.concatenate([U, D, D, U], 1), np.concatenate([hU, D, D, U], 1), np.concatenate([D, U, D, U], 1),
             np.concatenate([U, D, U, D], 1), np.concatenate([hU, D, hU, D], 1)]
    for n in range(4):
        sl = slice(32 * n, 32 * n + 32)
        tiles.append(np.concatenate([hU[:, sl], D[:, sl]] * 8, 1))
    for i, t in enumerate(tiles):
        cst[:, 320 + 512 * i:320 + 512 * (i + 1)] = t
    return cst, hv


def _tile_w(w, nk):
    ncols = w.shape[1]
    return np.ascontiguousarray(w.reshape(nk, 128, ncols // 128, 128).transpose(2, 1, 0, 3))


_PROG = {}


def kernel(x, c, w_ada, b_ada, norm_w, w_in, conv_w, q_norm_w, k_norm_w, w_br_conv, w_br_attn, w_out):
    x = np.asarray(x, np.float32)
    c = np.asarray(c, np.float32)
    wada_t = _tile_w(np.asarray(w_ada, np.float32)[0], 8)
    win_t = _tile_w(np.asarray(w_in, np.float32)[0], 8)
    wbc_t = _tile_w(np.asarray(w_br_conv, np.float32)[0], 8)
    wba_t = _tile_w(np.asarray(w_br_attn, np.float32)[0], 4)
    wout = np.ascontiguousarray(np.asarray(w_out, np.float32)[0])
    rows = np.concatenate([np.asarray(b_ada, np.float32)[0], np.asarray(norm_w, np.float32)[0],
                           np.asarray(q_norm_w, np.float32)[0], np.asarray(k_norm_w, np.float32)[0]])[None, :]
    rows = np.ascontiguousarray(rows)
    cw = np.asarray(conv_w, np.float32)[0]
    in_maps = []
    for core in range(NCORES):
        b, q = core // 4, core % 4
        t0 = q * TOK
        cst, hv = _consts(core)
        xcat = np.zeros((2 * TOK, 1024), np.float32)
        if q > 0:
            xcat[0:TOK] = x[b, t0 - TOK:t0]
        xcat[TOK:] = x[b, t0:t0 + TOK]
        small = np.zeros((128, 32), np.float32)
        small[:, 0:24] = cw.reshape(3, 8, 128).transpose(2, 1, 0).reshape(128, 24)
        small[:, 24] = np.tile(np.asarray(q_norm_w, np.float32)[0], 2)
        small[:, 25] = np.tile(np.asarray(k_norm_w, np.float32)[0], 2)
        small[:, 26] = hv
        in_maps.append({
            "x": xcat.reshape(32, 128, 1024),
            "ccol": np.ascontiguousarray(c[b].reshape(8, 128).T),
            "cst": cst, "small": small, "rows": rows,
            "wada": wada_t, "win": win_t, "wbc": wbc_t, "wba": wba_t, "wout": wout,
        })
    if "nc" not in _PROG:
        _PROG["nc"] = build_program()
    res = run_bass_kernel_spmd(_PROG["nc"], in_maps[:NRUN], core_ids=list(range(NRUN)))
    if DEBUG:
        _PROG["res"] = res
    out = np.zeros((2, 4 * TOK, 1024), np.float32)
    for core in range(NRUN):
        b, q = core // 4, core % 4
        out[b, q * TOK:(q + 1) * TOK] = np.asarray(res.results[core]["out"]).reshape(TOK, 1024)
    return out
```

```python
import numpy as np
from contextlib import ExitStack
import concourse.bass as bass
import concourse.mybir as mybir
from concourse.bass_utils import run_bass_kernel_spmd

F32 = mybir.dt.float32
BF16 = mybir.dt.bfloat16
AF = mybir.ActivationFunctionType
ALU = mybir.AluOpType
AX = mybir.AxisListType

ENGS = ("pe", "act", "dve", "pool", "sp")
NCORES = 8
TOK = 2048
EPS = 1e-6
NCST = 320 + 9 * 512
M_T0, M_T0H, M_T1, M_T2, M_T2H, M_T3 = 0, 1, 2, 3, 4, 5
DEBUG = False
STOP = 9
SUB = 9
NRUN = NCORES


class Tok:
    __slots__ = ("name", "w", "r", "wd", "rd", "disjoint", "psum")

    def __init__(self, name="", disjoint=False, psum=False):
        self.name = name
        self.psum = psum
        self.w = {}
        self.r = {}
        self.wd = []
        self.rd = []
        self.disjoint = disjoint


class Op:
    __slots__ = ("eng", "fn", "deps", "dma", "semkey", "ticket", "sig")


class Sched:
    def __init__(self, nc, es, n_dma_sems=28):
        self.nc = nc
        self.sems = {}
        for e in ENGS:
            self.sems[e] = es.enter_context(nc.semaphore("s_" + e))
        self.ndma = n_dma_sems
        for i in range(n_dma_sems):
            self.sems[("d", i)] = es.enter_context(nc.semaphore("s_d%d" % i))
        self.semval = {k: 0 for k in self.sems}
        self.known = {e: {} for e in ENGS}
        self.dma_rr = 0
        self.dma_last = {}
        self.toks = []
        self.ops = {e: [] for e in ENGS}
        self.allops = []

    def tok(self, name="", disjoint=False, psum=False):
        t = Tok(name, disjoint, psum)
        self.toks.append(t)
        return t

    def add(self, eng, fn, reads=(), writes=(), dma=False):
        op = Op()
        op.eng = eng
        op.fn = fn
        op.dma = dma
        op.sig = False
        op.ticket = None
        deps = set()
        for t in reads:
            deps.update(t.w.values())
            deps.update(t.wd)
            if t.psum:
                deps.update(o for en, o in t.r.items() if en != eng)
        for t in writes:
            deps.update(t.r.values())
            deps.update(t.rd)
            if not t.disjoint:
                deps.update(t.w.values())
                deps.update(t.wd)
        if eng == "pe" and not dma:
            deps = {d for d in deps if not (d.eng == "pe" and not d.dma)}
        if dma:
            i = self.dma_rr
            self.dma_rr = (self.dma_rr + 1) % self.ndma
            op.semkey = ("d", i)
            prev = self.dma_last.get(i)
            if prev is not None:
                deps.add(prev)
            self.dma_last[i] = op
        else:
            op.semkey = eng
        deps.discard(op)
        op.deps = deps
        for d in deps:
            d.sig = True
        for t in reads:
            if dma:
                t.rd.append(op)
            else:
                t.r[eng] = op
        for t in writes:
            if not t.disjoint:
                t.w = {}
                t.wd = []
                t.r = {}
                t.rd = []
            if dma:
                t.wd.append(op)
            else:
                t.w[eng] = op
        self.ops[eng].append(op)
        self.allops.append(op)
        return op

    def wait_all(self, eng, ops):
        op = Op()
        op.eng = eng
        op.fn = None
        op.dma = False
        op.sig = False
        op.ticket = None
        op.semkey = eng
        op.deps = set(o for o in ops if o is not None)
        for d in op.deps:
            d.sig = True
        self.ops[eng].append(op)
        return op

    def emit(self, final=False):
        nc = self.nc
        for e in ENGS:
            pend = [o for o in self.ops[e] if o.dma]
            if pend:
                self.wait_all(e, pend)
        for op in self.allops:
            if op.sig and op.fn is not None:
                self.semval[op.semkey] += 16 if op.dma else 1
                op.ticket = self.semval[op.semkey]
        sched = self

        def run(ename, eng):
            known = sched.known[ename]
            for op in sched.ops[ename]:
                need = {}
                for d in op.deps:
                    assert d.ticket is not None, (ename, d.eng)
                    if need.get(d.semkey, 0) < d.ticket:
                        need[d.semkey] = d.ticket
                for k, v in need.items():
                    if known.get(k, 0) < v:
                        eng.wait_ge(sched.sems[k], v)
                        known[k] = v
                if op.fn is None:
                    continue
                ins = op.fn(eng)
                if op.sig:
                    ins.then_inc(sched.sems[op.semkey], 16 if op.dma else 1)

        with nc.Block(no_gpsimd_drain=True) as block:
            @block.tensor
            def _(e):
                run("pe", e)

            @block.scalar
            def _(e):
                run("act", e)

            @block.vector
            def _(e):
                run("dve", e)

            @block.gpsimd
            def _(e):
                run("pool", e)

            @block.sync
            def _(e):
                run("sp", e)
        for t in self.toks:
            t.w = {}
            t.r = {}
            t.wd = []
            t.rd = []
        self.toks = []
        self.ops = {e: [] for e in ENGS}
        self.allops = []


def build_program():
    nc = bass.Bass("TRN2", target_bir_lowering=False)
    dt = nc.dram_tensor
    x_d = dt("x", [32, 128, 1024], F32, kind="ExternalInput").ap()
    ccol_d = dt("ccol", [128, 8], F32, kind="ExternalInput").ap()
    cst_d = dt("cst", [128, NCST], F32, kind="ExternalInput").ap()
    small_d = dt("small", [128, 32], F32, kind="ExternalInput").ap()
    rows_d = dt("rows", [1, 4224], F32, kind="ExternalInput").ap()
    wada_d = dt("wada", [24, 128, 8, 128], F32, kind="ExternalInput").ap()
    win_d = dt("win", [88, 128, 8, 128], F32, kind="ExternalInput").ap()
    wbc_d = dt("wbc", [8, 128, 8, 128], F32, kind="ExternalInput").ap()
    wba_d = dt("wba", [8, 128, 4, 128], F32, kind="ExternalInput").ap()
    wout_d = dt("wout", [1024, 1024], F32, kind="ExternalInput").ap()
    out_d = dt("out", [16, 128, 1024], F32, kind="ExternalOutput").ap()
    if DEBUG:
        d_hT = dt("d_hT", [128, 8, TOK], BF16, kind="ExternalOutput").ap()
        d_hh = dt("d_hh", [128, 8, TOK], BF16, kind="ExternalOutput").ap()
        d_acsh = dt("d_acsh", [128, 16], F32, kind="ExternalOutput").ap()
        d_gate = dt("d_gate", [128, 1024], F32, kind="ExternalOutput").ap()
        d_negc = dt("d_negc", [128, 1], F32, kind="ExternalOutput").ap()
        d_yb = dt("d_yb", [128, 4, TOK], BF16, kind="ExternalOutput").ap()
        d_ya = dt("d_ya", [128, 8, TOK], BF16, kind="ExternalOutput").ap()
        d_mg = dt("d_mg", [128, 8, TOK], BF16, kind="ExternalOutput").ap()
        d_Q = [[dt("d_Q%d_%d" % (g, hd), [128, TOK], BF16, kind="ExternalOutput").ap() for hd in range(2)] for g in range(3)]
        d_K = [dt("d_K%d" % g, [128, (128, 512, 2048)[g] + TOK], BF16, kind="ExternalOutput").ap() for g in range(3)]
        d_V = [dt("d_V%d" % g, [128, (1, 4, 16)[g] + 16, 128], BF16, kind="ExternalOutput").ap() for g in range(3)]
        d_zs = dt("d_zs", [128, TOK], BF16, kind="ExternalOutput").ap()

    with ExitStack() as es:
        S = Sched(nc, es)

        def sbt(stack, name, shape, dtype):
            return stack.enter_context(nc.sbuf_tensor("sb_" + name, shape, dtype))

        def pst(stack, name, shape, dtype):
            return stack.enter_context(nc.psum_tensor("ps_" + name, shape, dtype))

        cst = sbt(es, "cst", [128, NCST], BF16)
        ident = cst[:, 0:128]
        bones = cst[:, 128:256]
        twos = cst[:, 256:320]

        def mask(i):
            return cst[:, 320 + 512 * i: 320 + 512 * (i + 1)]

        small = sbt(es, "small", [128, 32], F32)
        cwh = sbt(es, "cwh", [128, 24], F32)
        wqk = sbt(es, "wqk", [128, 1], F32)
        negc = sbt(es, "negc", [128, 1], F32)
        epsc = sbt(es, "epsc", [128, 1], F32)
        acsh = sbt(es, "acsh", [128, 16], F32)
        gate = sbt(es, "gate", [128, 1024], F32)
        hT = sbt(es, "hT", [128, 8, TOK], BF16)
        bufA = sbt(es, "bufA", [128, 8, TOK], BF16)
        hh2 = sbt(es, "hh2", [128, 8, 2], BF16)
        yb = sbt(es, "yb", [128, 4, TOK], BF16)
        NW = 10
        wring = [sbt(es, "wr%d" % i, [128, 8, 128], BF16) for i in range(NW)]
        wtok = [None] * NW
        wstate = {"i": 0}

        wpre = {}

        def wload(src, nk=8, key=None, prefetch=False):
            if key is not None and key in wpre:
                i = wpre.pop(key)
                wtok[i] = S.tok("w%d" % i)
                return wring[i], wtok[i]
            i = wstate["i"]
            wstate["i"] = (i + 1) % NW
            if wtok[i] is None or wtok[i] not in S.toks:
                wtok[i] = S.tok("w%d" % i)
            slot = wring[i]
            S.add("pool", lambda e: e.dma_start(out=slot[:, 0:nk, :], in_=src), writes=[wtok[i]], dma=True)
            if prefetch:
                wpre[key] = i
            return slot, wtok[i]

        def unit(ps_ap, ps_tok, lhs_fn, rhs_fn, nk, reads):
            for kt in range(nk):
                la, ra = lhs_fn(kt), rhs_fn(kt)
                S.add("pe", lambda e, kt=kt, la=la, ra=ra: e.matmul(ps_ap, lhsT=la, rhs=ra,
                                                                    start=(kt == 0), stop=(kt == nk - 1)),
                      reads=reads, writes=[ps_tok])

        with ExitStack() as p0:
            ccol = sbt(p0, "ccol", [128, 8], F32)
            th8 = sbt(p0, "th8", [128, 8], F32)
            scb = sbt(p0, "scb", [128, 8], BF16)
            modrow = sbt(p0, "modrow", [1, 3072], F32)
            nwrow = sbt(p0, "nwrow", [1, 1024], F32)
            arow = sbt(p0, "arow", [1, 1024], F32)
            qkrow = sbt(p0, "qkrow", [1, 128], F32)
            prow = sbt(p0, "prow", [1, 64], F32)
            c11 = sbt(p0, "c11", [1, 2], F32)
            onesf = sbt(p0, "onesf", [1, 128], F32)
            xs = [sbt(p0, "xs%d" % i, [128, 1024], F32) for i in range(4)]
            xh = [sbt(p0, "xh%d" % i, [128, 1024], BF16) for i in range(8)]
            junk = sbt(p0, "junk", [128, 1024], BF16)
            ss = sbt(p0, "ss", [128, 32], F32)
            vv = sbt(p0, "vv", [128, 32], F32)
            lv = sbt(p0, "lv", [128, 32], F32)
            rstd = sbt(p0, "rstd", [128, 32], F32)
            ps_row = pst(p0, "ps_row", [128, 512], F32)
            ps_col = pst(p0, "ps_col", [128, 512], F32)
            ps_g = [pst(p0, "ps_g%d" % i, [128, 512], F32) for i in range(2)]
            psT = [pst(p0, "psT%d" % i, [128, 1024], BF16) for i in range(4)]

            t_cst, t_small, t_ccol, t_modrow, t_nw, t_qk = (S.tok() for _ in range(6))
            t_cst.disjoint = True
            S.add("pool", lambda e: e.dma_start(out=cst[:, 0:320], in_=cst_d[:, 0:320]), writes=[t_cst], dma=True)
            S.add("sp", lambda e: e.dma_start(out=small[:], in_=small_d[:]), writes=[t_small], dma=True)
            S.add("sp", lambda e: e.dma_start(out=ccol[:], in_=ccol_d[:]), writes=[t_ccol], dma=True)
            S.add("sp", lambda e: e.dma_start(out=modrow[:], in_=rows_d[:, 0:3072]), writes=[t_modrow], dma=True)
            S.add("sp", lambda e: e.dma_start(out=nwrow[:], in_=rows_d[:, 3072:4096]), writes=[t_nw], dma=True)
            S.add("sp", lambda e: e.dma_start(out=qkrow[:], in_=rows_d[:, 4096:4224]), writes=[t_qk], dma=True)
            t_ones = S.tok()
            S.add("dve", lambda e: e.memset(onesf[:], 1.0), writes=[t_ones])
            S.add("dve", lambda e: e.memset(epsc[:], EPS))

            t_th8, t_scb = S.tok(), S.tok()
            S.add("act", lambda e: e.activation(out=th8[:], in_=ccol[:], func=AF.Tanh, scale=0.5),
                  reads=[t_ccol], writes=[t_th8])
            S.add("dve", lambda e: e.tensor_scalar(out=th8[:], in0=th8[:], scalar1=0.5, scalar2=0.5,
                                                   op0=ALU.mult, op1=ALU.add), reads=[t_th8], writes=[t_th8])
            S.add("dve", lambda e: e.tensor_tensor(out=scb[:], in0=th8[:], in1=ccol[:], op=ALU.mult),
                  reads=[t_th8, t_ccol], writes=[t_scb])

            t_cwh, t_wqk = S.tok(), S.tok()
            S.add("dve", lambda e: e.tensor_scalar(out=cwh[:], in0=small[:, 0:24], scalar1=0.5, scalar2=None,
                                                   op0=ALU.mult), reads=[t_small], writes=[t_cwh])
            S.add("dve", lambda e: e.scalar_tensor_tensor(out=wqk[:], in0=small[:, 24:25], scalar=0.125,
                                                          in1=small[:, 25:26], op0=ALU.mult, op1=ALU.mult),
                  reads=[t_small], writes=[t_wqk])
            t_prow, t_c11 = S.tok(), S.tok()
            S.add("dve", lambda e: e.tensor_tensor(out=prow[:], in0=qkrow[:, 0:64], in1=qkrow[:, 64:128], op=ALU.mult),
                  reads=[t_qk], writes=[t_prow])
            S.add("dve", lambda e: e.reduce_max(out=c11[:, 0:1], in_=prow[:], axis=AX.X, apply_absolute_value=True),
                  reads=[t_prow], writes=[t_c11])
            S.add("dve", lambda e: e.tensor_scalar(out=c11[:, 1:2], in0=c11[:, 0:1], scalar1=-8.0, scalar2=None,
                                                   op0=ALU.mult), reads=[t_c11], writes=[t_c11])

            t_xs = [S.tok() for _ in range(4)]
            t_xh = [S.tok() for _ in range(8)]
            t_junk = S.tok(disjoint=True)
            t_psT = [S.tok(psum=True) for _ in range(4)]
            t_h3 = S.tok(disjoint=True)

            def front(gi):
                for t in range(4):
                    i = 4 * gi + t
                    xb = xs[t]
                    hb = xh[4 * (gi % 2) + t]
                    t_stat = S.tok()
                    S.add("sp", lambda e, i=i, xb=xb: e.dma_start(out=xb[:], in_=x_d[i]), writes=[t_xs[t]], dma=True)
                    S.add("act", lambda e, i=i, xb=xb: e.activation(out=junk[:], in_=xb[:], func=AF.Square,
                                                                    accum_out=ss[:, i:i + 1]),
                          reads=[t_xs[t]], writes=[t_junk, t_stat])
                    S.add("dve", lambda e, i=i: e.tensor_scalar(out=vv[:, i:i + 1], in0=ss[:, i:i + 1], scalar1=1.0 / 1024,
                                                                scalar2=EPS, op0=ALU.mult, op1=ALU.add),
                          reads=[t_stat], writes=[t_stat])
                    S.add("act", lambda e, i=i: e.activation(out=lv[:, i:i + 1], in_=vv[:, i:i + 1], func=AF.Ln),
                          reads=[t_stat], writes=[t_stat])
                    S.add("act", lambda e, i=i: e.activation(out=rstd[:, i:i + 1], in_=lv[:, i:i + 1], func=AF.Exp, scale=-0.5),
                          reads=[t_stat], writes=[t_stat])
                    S.add("dve", lambda e, i=i, xb=xb, hb=hb: e.tensor_scalar(out=hb[:], in0=xb[:], scalar1=rstd[:, i:i + 1],
                                                                              scalar2=None, op0=ALU.mult),
                          reads=[t_xs[t], t_stat], writes=[t_xh[4 * (gi % 2) + t]])

            def back(gi):
                dstbuf = bufA if gi < 4 else hT
                c0 = (gi % 4) * 512
                wr = [t_h3] if gi == 3 else []
                for j in range(4):
                    for kt in (2 * j, 2 * j + 1):
                        for t in range(4):
                            hi = 4 * (gi % 2) + t
                            S.add("pe", lambda e, j=j, kt=kt, t=t, hi=hi: e.transpose(
                                psT[j][:, (kt % 2) * 512 + t * 128:(kt % 2) * 512 + (t + 1) * 128],
                                xh[hi][:, kt * 128:(kt + 1) * 128], ident),
                                reads=[t_xh[hi], t_cst], writes=[t_psT[j]])
                    for kt in (2 * j, 2 * j + 1):
                        dst = dstbuf[:, kt, c0:c0 + 512]
                        src = psT[j][:, (kt % 2) * 512:(kt % 2) * 512 + 512]
                        if j % 2 == 0:
                            S.add("dve", lambda e, dst=dst, src=src, kt=kt: e.tensor_scalar(
                                out=dst, in0=src, scalar1=acsh[:, kt:kt + 1], scalar2=acsh[:, 8 + kt:9 + kt],
                                op0=ALU.mult, op1=ALU.add), reads=[t_psT[j], t_acsh], writes=wr)
                        else:
                            S.add("act", lambda e, dst=dst, src=src, kt=kt: e.activation(
                                out=dst, in_=src, func=AF.Identity, scale=acsh[:, kt:kt + 1], bias=acsh[:, 8 + kt:9 + kt]),
                                reads=[t_psT[j], t_acsh], writes=wr)
                if gi == 3:
                    S.add("dve", lambda e: e.tensor_copy(out=hh2[:], in_=bufA[:, :, TOK - 2:TOK]), reads=[t_h3])

            t_acsh = S.tok()
            front(0)
            front(1)

            t_psrow = S.tok(psum=True)

            def mod_chunk(ci):
                for j in range(4):
                    ct = 4 * ci + j
                    slot, wt = wload(wada_d[ct])
                    unit(ps_row[0:1, j * 128:(j + 1) * 128], t_psrow,
                         lambda kt: scb[:, kt:kt + 1], lambda kt, slot=slot: slot[:, kt, :], 8, [wt, t_scb])
                c0 = ci * 512
                S.add("dve", lambda e, c0=c0: e.tensor_tensor(out=modrow[:, c0:c0 + 512], in0=modrow[:, c0:c0 + 512],
                                                              in1=ps_row[0:1, :], op=ALU.add),
                      reads=[t_psrow, t_modrow], writes=[t_modrow])

            for ci in (2, 3, 0, 1):
                mod_chunk(ci)
            t_arow = S.tok()
            S.add("dve", lambda e: e.scalar_tensor_tensor(out=arow[:], in0=modrow[:, 1024:2048], scalar=1.0, in1=nwrow[:],
                                                          op0=ALU.add, op1=ALU.mult),
                  reads=[t_modrow, t_nw], writes=[t_arow])
            t_pscol, t_negc = S.tok(psum=True), S.tok()
            for kt in range(8):
                S.add("pe", lambda e, kt=kt: e.matmul(ps_col[:, kt:kt + 1], lhsT=arow[0:1, kt * 128:(kt + 1) * 128],
                                                      rhs=onesf[0:1, 0:1], start=True, stop=True),
                      reads=[t_arow, t_ones], writes=[t_pscol])
                S.add("pe", lambda e, kt=kt: e.matmul(ps_col[:, 8 + kt:9 + kt], lhsT=modrow[0:1, kt * 128:(kt + 1) * 128],
                                                      rhs=onesf[0:1, 0:1], start=True, stop=True),
                      reads=[t_modrow, t_ones], writes=[t_pscol])
            S.add("pe", lambda e: e.matmul(ps_col[:, 16:17], lhsT=onesf[0:1, 0:128], rhs=c11[0:1, 1:2],
                                           start=True, stop=True), reads=[t_c11, t_ones], writes=[t_pscol])
            S.add("dve", lambda e: e.tensor_copy(out=acsh[:], in_=ps_col[:, 0:16]), reads=[t_pscol], writes=[t_acsh])
            S.add("dve", lambda e: e.tensor_copy(out=negc[:], in_=ps_col[:, 16:17]), reads=[t_pscol], writes=[t_negc])

            for gi in range(8):
                back(gi)
                if gi + 2 < 8:
                    front(gi + 2)

            for ci in (4, 5):
                mod_chunk(ci)
            t_psg, t_gate = S.tok(psum=True), S.tok(disjoint=True)
            for h in range(2):
                S.add("pe", lambda e, h=h: e.matmul(ps_g[h][:, :], lhsT=onesf[0:1, 0:128],
                                                    rhs=modrow[0:1, 2048 + 512 * h:2560 + 512 * h], start=True, stop=True),
                      reads=[t_modrow, t_ones], writes=[t_psg])
                S.add("act", lambda e, h=h: e.activation(out=gate[:, 512 * h:512 * h + 512], in_=ps_g[h][:, :],
                                                         func=AF.Copy, scale=0.5), reads=[t_psg], writes=[t_gate])
            wload(win_d[44], key=("win", 44), prefetch=True)
            S.emit()
        def dump0():
            for dst, src in ((d_hT, hT), (d_hh, bufA), (d_acsh, acsh), (d_gate, gate), (d_negc, negc)):
                S.add("sp", lambda e, dst=dst, src=src: e.dma_start(out=dst[:], in_=src[:]), dma=True)

        if STOP <= 0:
            if DEBUG:
                dump0()
                S.emit()
            return nc

        HG = (128, 512, 2048)
        DIL = (1, 4, 16)
        with ExitStack() as p1:
            if DEBUG:
                dump0()
            Qn = [sbt(p1, "Qn%d" % g, [128, TOK], BF16) for g in range(3)]
            print("p1 sbuf remaining before rest", nc.sbuf_bytes_remaining)
            Kn = [sbt(p1, "Kn%d" % g, [128, HG[g] + TOK], BF16) for g in range(3)]
            Vt = [sbt(p1, "Vt%d" % g, [128, HG[g] // 128 + 16, 128], BF16) for g in range(3)]
            sq = [sbt(p1, "sq%d" % i, [128, 512], BF16) for i in range(2)]
            lnv = [sbt(p1, "lnv%d" % i, [128, 512], F32) for i in range(2)]
            NE = 6
            Eb = [sbt(p1, "Eb%d" % i, [128, 512], BF16) for i in range(NE)]
            Pb = [sbt(p1, "Pb%d" % i, [128, 512], BF16) for i in range(NE)]
            zs = sbt(p1, "zs", [128, TOK], BF16)
            thz = [sbt(p1, "thz%d" % i, [128, 512], F32) for i in range(2)]
            rec = [sbt(p1, "rec%d" % i, [128, 512], F32) for i in range(2)]
            ton = [sbt(p1, "ton%d" % i, [128, 512], F32) for i in range(2)]
            bk = [pst(p1, "bk%d" % i, [128, 512], F32) for i in range(8)]
            tbk = [S.tok(psum=True) for _ in range(8)]
            NPP = 5
            pp, t_pp = bk[0:5], tbk[0:5]
            pq, t_pq = bk[5:7], tbk[5:7]
            NS = 4
            pS, t_pS = bk[0:4], tbk[0:4]
            pOs, t_pOs = [bk[4], bk[6]], [tbk[4], tbk[6]]
            pMs, t_pMs = [bk[5], bk[7]], [tbk[5], tbk[7]]

            t_sq = [S.tok(), S.tok()]
            t_lnv = [S.tok(), S.tok()]
            t_Q = [S.tok(disjoint=True) for _ in range(3)]
            t_K = [S.tok(disjoint=True) for _ in range(3)]
            t_V = [S.tok(disjoint=True) for _ in range(3)]
            t_E = [S.tok() for _ in range(NE)]
            t_P = [S.tok() for _ in range(NE)]
            t_zs = S.tok(disjoint=True)
            t_thz = [S.tok(), S.tok()]
            t_rec = [S.tok(), S.tok()]
            t_ton = [S.tok(), S.tok()]
            t_yb = S.tok(disjoint=True)
            ucnt = {"u": 0}

            def qk_unit(slot, wt, src_fn, N, dst_ap, dst_tok, scalar, r, dst_ap2=None):
                u = ucnt["u"]
                ucnt["u"] += 1
                pb = u % NPP
                qc = ucnt.get("q", 0)
                ucnt["q"] = qc + 1
                b = qc % 2
                ps = pp[pb][:, 0:N]
                unit(ps, t_pp[pb], lambda kt: slot[:, kt, :], src_fn, 8, [wt])
                S.add("act", lambda e: e.activation(out=sq[b][:, 0:N], in_=ps, func=AF.Square),
                      reads=[t_pp[pb]], writes=[t_sq[b]])
                return lambda: qk_part2(ps, pb, b, N, dst_ap, dst_tok, scalar, r, dst_ap2)

            def qk_part2(ps, pb, b, N, dst_ap, dst_tok, scalar, r, dst_ap2):
                S.add("pe", lambda e: e.matmul(pq[b][:, 0:N], lhsT=bones, rhs=sq[b][:, 0:N], start=True, stop=True),
                      reads=[t_sq[b]], writes=[t_pq[b]])
                S.add("act", lambda e: e.activation(out=lnv[b][:, 0:N], in_=pq[b][:, 0:N], func=AF.Ln, bias=EPS),
                      reads=[t_pq[b]], writes=[t_lnv[b]])
                S.add("act", lambda e: e.activation(out=lnv[b][:, 0:N], in_=lnv[b][:, 0:N], func=AF.Exp, scale=-0.5),
                      reads=[t_lnv[b]], writes=[t_lnv[b]])
                if dst_ap2 is None:
                    S.add("dve", lambda e: e.scalar_tensor_tensor(out=dst_ap, in0=ps_view(ps, r), scalar=scalar,
                                                                  in1=ps_view(lnv[b][:, 0:N], r),
                                                                  op0=ALU.mult, op1=ALU.mult),
                          reads=[t_pp[pb], t_lnv[b]], writes=[dst_tok])
                else:
                    for hd, d_ap in enumerate((dst_ap, dst_ap2)):
                        rows = slice(64 * hd, 64 * hd + 64)
                        S.add("dve", lambda e, rows=rows, d_ap=d_ap: e.scalar_tensor_tensor(
                            out=d_ap, in0=ps_view(pp[pb][rows, 0:N], r), scalar=scalar[rows, :],
                            in1=ps_view(lnv[b][rows, 0:N], r), op0=ALU.mult, op1=ALU.mult),
                            reads=[t_pp[pb], t_lnv[b]], writes=[dst_tok])

            view_state = {}

            def ps_view(src, r):
                if r == 1:
                    return src
                return src.rearrange("p (a r) -> p r a", r=r)

            def dst_view(buf, base, Lsub, r, a0, na):
                if r == 1:
                    return buf[:, base + a0: base + a0 + na]
                return buf[:, base:base + r * Lsub].rearrange("p (r l) -> p r l", r=r)[:, :, a0:a0 + na]

            def load_pair(m):
                W = {}
                for g in range(3):
                    W[("k", g)] = wload(win_d[44 + 4 * g + m], key=("win", 44 + 4 * g + m))
                    W[("q", g)] = wload(win_d[32 + 4 * g + m], key=("win", 32 + 4 * g + m))
                    W[("v", g)] = wload(win_d[56 + 4 * g + m], key=("win", 56 + 4 * g + m))
                W["z"] = wload(win_d[68 + m])
                return W

            Wnext = load_pair(0)
            t_msk = S.tok(disjoint=True)
            for c0 in range(320, NCST, 1152):
                S.add("pool", lambda e, c0=c0: e.dma_start(out=cst[:, c0:c0 + 1152], in_=cst_d[:, c0:c0 + 1152]),
                      writes=[t_msk], dma=True)
            for m in range(4):
                W = Wnext
                qk_list, v_list, z_list = [], [], []
                for g in range(3):
                    r = DIL[g]
                    H = HG[g]
                    L = TOK // r
                    slotk, wtk = W[("k", g)]
                    nh = max(1, H // 512)
                    for u in range(nh):
                        N = min(512, H)
                        c0 = TOK - H + 512 * u
                        na = N // r
                        dst = dst_view(Kn[g], 0, 128, r, (512 * u) // r, na)
                        qk_list.append(lambda slotk=slotk, wtk=wtk, c0=c0, N=N, dst=dst, g=g, r=r: qk_unit(
                            slotk, wtk, lambda kt: bufA[:, kt, c0:c0 + N], N, dst, t_K[g], 1.0, r))
                    for n in range(4):
                        dst = dst_view(Kn[g], H, L, r, (512 * n) // r, 512 // r)
                        qk_list.append(lambda slotk=slotk, wtk=wtk, n=n, dst=dst, g=g, r=r: qk_unit(
                            slotk, wtk, lambda kt: hT[:, kt, 512 * n:512 * n + 512], 512, dst, t_K[g], 1.0, r))
                    slotq, wtq = W[("q", g)]
                    for n in range(4):
                        dstq = dst_view(Qn[g], 0, L, r, (512 * n) // r, 512 // r)
                        qk_list.append(lambda slotq=slotq, wtq=wtq, n=n, dstq=dstq, g=g, r=r: qk_unit(
                            slotq, wtq, lambda kt: hT[:, kt, 512 * n:512 * n + 512], 512, dstq, t_Q[g], wqk[:, 0:1], r))
                    slotv, wtv = W[("v", g)]
                    nhb = H // 128
                    nkb = nhb + 16

                    def v_group(kb0, g=g, r=r, H=H, L=L, slotv=slotv, wtv=wtv, nhb=nhb, nkb=nkb):
                        u = ucnt["u"]
                        ucnt["u"] += 1
                        b = u % NPP
                        nb = min(4, nkb - kb0)
                        for j in range(nb):
                            kb = kb0 + j
                            if kb < nhb:
                                srcbuf, start = bufA, TOK - H + kb
                            else:
                                o = kb - nhb
                                rr, bb = o // (L // 128), o % (L // 128)
                                srcbuf, start = hT, 128 * bb * r + rr
                            if r == 1:
                                lhs_fn = lambda kt, srcbuf=srcbuf, start=start: srcbuf[:, kt, start:start + 128]
                            else:
                                lhs_fn = lambda kt, srcbuf=srcbuf, start=start, r=r: \
                                    srcbuf[:, kt, start - (start % r):start - (start % r) + 128 * r].rearrange(
                                        "p (a r) -> p r a", r=r)[:, start % r, :]
                            unit(pp[b][:, j * 128:(j + 1) * 128], t_pp[b], lhs_fn,
                                 lambda kt: slotv[:, kt, :], 8, [wtv])
                        S.add("dve", lambda e, b=b, nb=nb, g=g, kb0=kb0: e.tensor_copy(
                            out=Vt[g][:, kb0:kb0 + nb, :].rearrange("p a b -> p (a b)"), in_=pp[b][:, 0:nb * 128]),
                            reads=[t_pp[b]], writes=[t_V[g]])

                    for kb0 in range(0, nkb, 4):
                        v_list.append(lambda kb0=kb0, v_group=v_group: v_group(kb0))
                slotz, wtz = W["z"]

                def z_unit(n, slotz=slotz, wtz=wtz):
                    u = ucnt["u"]
                    ucnt["u"] += 1
                    b = u % NPP
                    zb = n % 2
                    unit(pp[b][:, :], t_pp[b], lambda kt: slotz[:, kt, :],
                         lambda kt: hT[:, kt, 512 * n:512 * n + 512], 8, [wtz])
                    S.add("act", lambda e, b=b, zb=zb: e.activation(out=thz[zb][:], in_=pp[b][:, :], func=AF.Tanh, scale=0.5),
                          reads=[t_pp[b]], writes=[t_thz[zb]])
                    S.add("dve", lambda e, b=b, n=n, zb=zb: e.scalar_tensor_tensor(
                        out=zs[:, 512 * n:512 * n + 512], in0=thz[zb][:], scalar=1.0, in1=pp[b][:, :],
                        op0=ALU.add, op1=ALU.mult), reads=[t_thz[zb], t_pp[b]], writes=[t_zs])

                for n in range(4):
                    z_list.append(lambda n=n, z_unit=z_unit: z_unit(n))
                light = v_list
                pend = None
                while qk_list or light:
                    cont = qk_list.pop(0)() if qk_list else None
                    if light:
                        light.pop(0)()
                        if pend is not None:
                            pend()
                            pend = None
                        if cont is not None:
                            cont()
                    else:
                        if pend is not None:
                            pend()
                        pend = cont
                if pend is not None:
                    pend()
                for zf in z_list:
                    zf()
                if m + 1 < 4:
                    Wnext = load_pair(m + 1)
                else:
                    for ti in (8, 16, 24, 0):
                        wload(win_d[ti], key=("win", ti), prefetch=True)

                def batches(n):
                    res = []
                    i0 = 4 * n
                    if n == 0:
                        prev = (0, 0, i0 * 128, 128, (0, 1))
                    else:
                        prev = (0, 128 + (i0 - 1) * 128, i0 * 128, 128, (0, 1))
                    last = (0, 128 + (i0 + 3) * 128, (i0 + 3) * 128, 128, (384, 1))
                    res.append((M_T0H if n == 0 else M_T0, [prev, last, (0, 128 + i0 * 128, i0 * 128, 256, (0, 1))]))
                    res.append((M_T1, [(0, 128 + (i0 + j) * 128, (i0 + j) * 128, 256, (j * 128, 1)) for j in (1, 2)]))
                    for rp in range(2):
                        pcs = []
                        for rr in (2 * rp, 2 * rp + 1):
                            if n == 0:
                                pcs.append((1, rr * 128, rr * 512, 128, (rr, 4)))
                            else:
                                pcs.append((1, 512 + rr * 512 + (n - 1) * 128, rr * 512 + n * 128, 128, (rr, 4)))
                            pcs.append((1, 512 + rr * 512 + n * 128, rr * 512 + n * 128, 128, (rr, 4)))
                        res.append((M_T2H if n == 0 else M_T2, pcs))
                    for rb in range(2):
                        pcs = []
                        for rr in range(8 * rb, 8 * rb + 8):
                            pcs.append((2, rr * 128, rr * 128 + 32 * n, 32, (rr, 16)))
                            pcs.append((2, 2048 + rr * 128, rr * 128 + 32 * n, 32, (rr, 16)))
                        res.append((M_T3 + n, pcs))
                    return res

                blist = []
                for n in range(4):
                    bl = batches(n)
                    for bi, (mi, pcs) in enumerate(bl):
                        blist.append((n, bi == 0, bi == len(bl) - 1, mi, pcs))
                NB = len(blist)

                def emit_S(bidx):
                    n, first, lastb, mi, pcs = blist[bidx]
                    off = 0
                    for (g, kcol, qcol, N, ospec) in pcs:
                        for hd in range(2):
                            sb3 = (2 * bidx + hd) % NS
                            rows = slice(64 * hd, 64 * hd + 64)
                            S.add("pe", lambda e, g=g, kcol=kcol, qcol=qcol, N=N, rows=rows, off=off, sb3=sb3: e.matmul(
                                pS[sb3][:, off:off + N], lhsT=Kn[g][rows, kcol:kcol + 128],
                                rhs=Qn[g][rows, qcol:qcol + N], start=True, stop=True),
                                reads=[t_K[g], t_Q[g]], writes=[t_pS[sb3]])
                        off += N
                    assert off == 512
                    for hd in range(2):
                        sb3 = (2 * bidx + hd) % NS
                        eb = (2 * bidx + hd) % NE
                        S.add("act", lambda e, sb3=sb3, eb=eb: e.activation(out=Eb[eb][:], in_=pS[sb3][:, :], func=AF.Exp,
                                                                            bias=negc[:, 0:1], scale=1.0),
                              reads=[t_pS[sb3]], writes=[t_E[eb]])
                        S.add("dve", lambda e, eb=eb, mi=mi: e.tensor_tensor(out=Pb[eb][:], in0=Eb[eb][:], in1=mask(mi),
                                                                             op=ALU.mult),
                              reads=[t_E[eb], t_msk], writes=[t_P[eb]])

                def emit_PV(bidx):
                    n, first, lastb, mi, pcs = blist[bidx]
                    pO, t_pO, pM, t_pM = pOs[n % 2], t_pOs[n % 2], pMs[n % 2], t_pMs[n % 2]
                    np_ = len(pcs)
                    for hd in range(2):
                        eb = (2 * bidx + hd) % NE
                        off = 0
                        for pi, (g, kcol, qcol, N, (ostart, ostep)) in enumerate(pcs):
                            vkb = kcol // 128
                            st = first and pi == 0
                            sp_ = lastb and pi == np_ - 1
                            rows = slice(64 * hd, 64 * hd + 64)

                            def oview(t, rows=rows, ostart=ostart, ostep=ostep, N=N):
                                if ostep == 1:
                                    return t[rows, ostart:ostart + N]
                                return t[rows, :].rearrange("p (a r) -> p r a", r=ostep)[:, ostart, 0:N]

                            S.add("pe", lambda e, g=g, vkb=vkb, hd=hd, off=off, N=N, eb=eb, st=st, sp_=sp_, oview=oview:
                                  e.matmul(oview(pO), lhsT=Vt[g][:, vkb, 64 * hd:64 * hd + 64],
                                           rhs=Pb[eb][:, off:off + N], start=st, stop=sp_, skip_group_check=True),
                                  reads=[t_P[eb], t_V[g]], writes=[t_pO])
                            S.add("pe", lambda e, hd=hd, off=off, N=N, eb=eb, st=st, sp_=sp_, oview=oview:
                                  e.matmul(oview(pM), lhsT=twos, rhs=Pb[eb][:, off:off + N], start=st, stop=sp_,
                                           skip_group_check=True),
                                  reads=[t_P[eb]], writes=[t_pM])
                            off += N
                    if lastb:
                        b = n % 2
                        S.add("act", lambda e, b=b, pM=pM: e.activation(out=rec[b][:], in_=pM[:, :], func=AF.Ln),
                              reads=[t_pM], writes=[t_rec[b]])
                        S.add("act", lambda e, b=b: e.activation(out=rec[b][:], in_=rec[b][:], func=AF.Exp, scale=-1.0),
                              reads=[t_rec[b]], writes=[t_rec[b]])
                        S.add("dve", lambda e, b=b, pO=pO: e.tensor_tensor(out=ton[b][:], in0=pO[:, :], in1=rec[b][:], op=ALU.mult),
                              reads=[t_pO, t_rec[b]], writes=[t_ton[b]])
                        S.add("pool", lambda e, b=b, n=n, m=m: e.tensor_tensor(
                            out=yb[:, m, 512 * n:512 * n + 512], in0=ton[b][:], in1=zs[:, 512 * n:512 * n + 512],
                            op=ALU.mult), reads=[t_ton[b], t_zs], writes=[t_yb])

                emit_S(0)
                emit_S(1)
                for bidx in range(NB):
                    if bidx + 2 < NB:
                        emit_S(bidx + 2)
                    emit_PV(bidx)
            if DEBUG and STOP <= 1:
                S.emit()
                for g in range(3):
                    for hd in range(2):
                        S.add("sp", lambda e, g=g, hd=hd: e.dma_start(out=d_Q[g][hd][:], in_=Qn[g][hd][:]), dma=True)
                    S.add("sp", lambda e, g=g: e.dma_start(out=d_K[g][:], in_=Kn[g][:]), dma=True)
                    S.add("sp", lambda e, g=g: e.dma_start(out=d_V[g][:], in_=Vt[g][:]), dma=True)
                S.add("sp", lambda e: e.dma_start(out=d_zs[:], in_=zs[:]), dma=True)
            S.emit()
        if STOP <= 1:
            if DEBUG:
                S.add("sp", lambda e: e.dma_start(out=d_yb[:], in_=yb[:]), dma=True)
                S.emit()
            return nc

        ya = bufA
        with ExitStack() as p2:
            if DEBUG:
                S.add("sp", lambda e: e.dma_start(out=d_yb[:], in_=yb[:]), dma=True)
            csb = [sbt(p2, "csb%d" % i, [128, 512], F32) for i in range(2)]
            ub = [sbt(p2, "ub%d" % i, [128, 514], F32) for i in range(2)]
            tb = [sbt(p2, "tb%d" % i, [128, 512], F32) for i in range(2)]
            thb = [sbt(p2, "thb%d" % i, [128, 512], F32) for i in range(2)]
            szb = [sbt(p2, "szb%d" % i, [128, 512], F32) for i in range(2)]
            bzb = [sbt(p2, "bzb%d" % i, [128, 512], F32) for i in range(2)]
            ch2 = sbt(p2, "ch2", [128, 2], F32)
            pc = [pst(p2, "pc%d" % i, [128, 512], F32) for i in range(6)]
            ph = pst(p2, "ph", [128, 512], F32)
            t_pc = [S.tok(psum=True) for _ in range(6)]
            t_ph = S.tok(psum=True)
            t_csb = [S.tok(), S.tok()]
            t_ub = [S.tok(), S.tok()]
            t_tb = [S.tok(), S.tok()]
            t_thb = [S.tok(), S.tok()]
            t_szb = [S.tok(), S.tok()]
            t_bzb = [S.tok(), S.tok()]
            t_ch2 = S.tok()
            t_ya = S.tok(disjoint=True)
            pcn = {"i": 0}

            def nextp():
                i = pcn["i"]
                pcn["i"] = (i + 1) % 6
                return pc[i], t_pc[i]

            it = 0

            def load_ct(ct):
                return [wload(win_d[t], key=("win", t)) for t in (8 + ct, 16 + ct, 24 + ct, ct)]

            Wn = load_ct(0)
            for ct in range(8):
                (sC, wC), (sX, wX), (sZ, wZ), (sB, wB) = Wn
                if ct + 1 < 8:
                    Wn = load_ct(ct + 1)
                if ct == 7:
                    wload(win_d[72], key=("win", 72), prefetch=True)
                    wload(wbc_d[0], key=("wbc", 0), prefetch=True)
                    wload(win_d[80], key=("win", 80), prefetch=True)
                    wload(wba_d[0], nk=4, key=("wba", 0), prefetch=True)
                unit(ph[:, 0:2], t_ph, lambda kt: sC[:, kt, :], lambda kt: hh2[:, kt, :], 8, [wC])
                unit(ph[:, 2:4], t_ph, lambda kt: sX[:, kt, :], lambda kt: hh2[:, kt, :], 8, [wX])
                S.add("act", lambda e: e.activation(out=ch2[:], in_=ph[:, 0:2], func=AF.Copy, scale=small[:, 26:27]),
                      reads=[t_ph], writes=[t_ch2])
                for n in range(4):
                    b = it % 2
                    it += 1
                    rhs_fn = lambda kt, n=n: hT[:, kt, 512 * n:512 * n + 512]
                    pC, tC = nextp()
                    unit(pC[:, :], tC, lambda kt: sC[:, kt, :], rhs_fn, 8, [wC])
                    S.add("act", lambda e, b=b, pC=pC: e.activation(out=csb[b][:], in_=pC[:, :], func=AF.Copy),
                          reads=[tC], writes=[t_csb[b]])
                    pX, tX = nextp()
                    unit(pX[:, :], tX, lambda kt: sX[:, kt, :], rhs_fn, 8, [wX])
                    if n == 0:
                        S.add("dve", lambda e, b=b: e.tensor_tensor(out=ub[b][:, 0:2], in0=ph[:, 2:4], in1=ch2[:], op=ALU.mult),
                              reads=[t_ph, t_ch2], writes=[t_ub[b]])
                    else:
                        S.add("dve", lambda e, b=b: e.tensor_copy(out=ub[b][:, 0:2], in_=ub[1 - b][:, 512:514]),
                              reads=[t_ub[1 - b]], writes=[t_ub[b]])
                    S.add("dve", lambda e, b=b, pX=pX: e.tensor_tensor(out=ub[b][:, 2:514], in0=pX[:, :], in1=csb[b][:], op=ALU.mult),
                          reads=[tX, t_csb[b], t_ub[b]], writes=[t_ub[b]])
                    S.add("dve", lambda e, b=b, ct=ct: e.tensor_scalar(out=tb[b][:], in0=ub[b][:, 2:514],
                                                                       scalar1=cwh[:, 3 * ct + 2:3 * ct + 3], scalar2=None,
                                                                       op0=ALU.mult), reads=[t_ub[b]], writes=[t_tb[b]])
                    S.add("dve", lambda e, b=b, ct=ct: e.scalar_tensor_tensor(out=tb[b][:], in0=ub[b][:, 1:513],
                                                                              scalar=cwh[:, 3 * ct + 1:3 * ct + 2], in1=tb[b][:],
                                                                              op0=ALU.mult, op1=ALU.add),
                          reads=[t_ub[b], t_tb[b]], writes=[t_tb[b]])
                    S.add("dve", lambda e, b=b, ct=ct: e.scalar_tensor_tensor(out=tb[b][:], in0=ub[b][:, 0:512],
                                                                              scalar=cwh[:, 3 * ct:3 * ct + 1], in1=tb[b][:],
                                                                              op0=ALU.mult, op1=ALU.add),
                          reads=[t_ub[b], t_tb[b]], writes=[t_tb[b]])
                    pZ, tZ = nextp()
                    unit(pZ[:, :], tZ, lambda kt: sZ[:, kt, :], rhs_fn, 8, [wZ])
                    S.add("act", lambda e, b=b, pZ=pZ: e.activation(out=thb[b][:], in_=pZ[:, :], func=AF.Tanh, scale=0.5),
                          reads=[tZ], writes=[t_thb[b]])
                    S.add("dve", lambda e, b=b, pZ=pZ: e.scalar_tensor_tensor(out=szb[b][:], in0=thb[b][:], scalar=1.0, in1=pZ[:, :],
                                                                              op0=ALU.add, op1=ALU.mult),
                          reads=[t_thb[b], tZ], writes=[t_szb[b]])
                    pB, tB = nextp()
                    unit(pB[:, :], tB, lambda kt: sB[:, kt, :], rhs_fn, 8, [wB])
                    S.add("dve", lambda e, b=b, pB=pB: e.tensor_tensor(out=bzb[b][:], in0=pB[:, :], in1=szb[b][:], op=ALU.mult),
                          reads=[tB, t_szb[b]], writes=[t_bzb[b]])
                    S.add("pool", lambda e, b=b, ct=ct, n=n: e.tensor_tensor(out=ya[:, ct, 512 * n:512 * n + 512], in0=bzb[b][:],
                                                                            in1=tb[b][:], op=ALU.mult),
                          reads=[t_bzb[b], t_tb[b]], writes=[t_ya])
            S.emit()
        if STOP <= 2:
            if DEBUG:
                S.add("sp", lambda e: e.dma_start(out=d_yb[:], in_=yb[:]), dma=True)
                S.add("sp", lambda e: e.dma_start(out=d_ya[:], in_=bufA[:]), dma=True)
                S.emit()
            return nc

        with ExitStack() as p34:
            mg = sbt(p34, "mg", [128, 8, TOK], BF16)
            wo = sbt(p34, "wo", [128, 8, 1024], BF16)
            xr = [sbt(p34, "xr%d" % i, [128, 1024], F32) for i in range(3)]
            tha = [sbt(p34, "tha%d" % i, [128, 512], F32) for i in range(2)]
            tgb = [sbt(p34, "tgb%d" % i, [128, 512], F32) for i in range(2)]
            ma = [sbt(p34, "ma%d" % i, [128, 512], F32) for i in range(2)]
            mb = [sbt(p34, "mb%d" % i, [128, 512], F32) for i in range(2)]
            tg = [sbt(p34, "tg%d" % i, [128, 1024], F32) for i in range(2)]
            ob = [sbt(p34, "ob%d" % i, [128, 1024], F32) for i in range(2)]
            pc = [pst(p34, "pd%d" % i, [128, 512], F32) for i in range(6)]
            po = [pst(p34, "po%d" % i, [128, 512], F32) for i in range(2)]
            if DEBUG:
                S.add("sp", lambda e: e.dma_start(out=d_ya[:], in_=ya[:]), dma=True)
            t_pc = [S.tok(psum=True) for _ in range(6)]
            t_po = [S.tok(psum=True) for _ in range(2)]
            t_tha = [S.tok(), S.tok()]
            t_tgb = [S.tok(), S.tok()]
            t_ma = [S.tok(), S.tok()]
            t_mb = [S.tok(), S.tok()]
            t_mg = [S.tok(disjoint=True), S.tok(disjoint=True)]
            t_wo = S.tok(disjoint=True)
            t_xr = [S.tok() for _ in range(3)]
            t_tg = [S.tok(disjoint=True), S.tok(disjoint=True)]
            t_ob = [S.tok() for _ in range(2)]
            pcn = {"i": 0}

            def nextp3():
                i = pcn["i"]
                pcn["i"] = (i + 1) % 6
                return pc[i], t_pc[i]

            def load_ft(ft):
                return [wload(win_d[72 + ft], key=("win", 72 + ft)), wload(wbc_d[ft], key=("wbc", ft)),
                        wload(win_d[80 + ft], key=("win", 80 + ft)), wload(wba_d[ft], nk=4, key=("wba", ft))]

            def xload(i):
                xb = i % 3
                S.add("sp", lambda e: e.dma_start(out=xr[xb][:], in_=x_d[16 + i]), writes=[t_xr[xb]], dma=True)

            def out_tile(i):
                b = i % 2
                xb = i % 3
                hf = i // 8
                for h in range(2):
                    unit(po[h][:, :], t_po[h], lambda kt: mg[:, kt, 128 * i:128 * i + 128],
                         lambda kt: wo[:, kt, 512 * h:512 * h + 512], 8, [t_mg[hf], t_wo])
                    S.add("dve", lambda e, h=h: e.tensor_tensor(out=tg[b][:, 512 * h:512 * h + 512], in0=po[h][:, :],
                                                                in1=gate[:, 512 * h:512 * h + 512], op=ALU.mult),
                          reads=[t_po[h]], writes=[t_tg[b]])
                S.add("pool", lambda e: e.tensor_tensor(out=ob[b][:], in0=tg[b][:], in1=xr[xb][:], op=ALU.add),
                      reads=[t_tg[b], t_xr[xb]], writes=[t_ob[b]])
                S.add("sp", lambda e: e.dma_start(out=out_d[i], in_=ob[b][:]), reads=[t_ob[b]], dma=True)
                if i + 3 < 16:
                    xload(i + 3)

            it = 0
            for half in range(2):
                Wn = load_ft(0)
                for ft in range(8):
                    (sGa, wGa), (sPa, wPa), (sGb, wGb), (sPb, wPb) = Wn
                    if ft + 1 < 8:
                        Wn = load_ft(ft + 1)
                    if half == 0 and ft == 1:
                        for kt in range(8):
                            S.add("pool", lambda e, kt=kt: e.dma_start(out=wo[:, kt, :], in_=wout_d[kt * 128:(kt + 1) * 128, :]),
                                  writes=[t_wo], dma=True)
                        for i in range(3):
                            xload(i)
                    for n in (2 * half, 2 * half + 1):
                        b = it % 2
                        it += 1
                        cs = slice(512 * n, 512 * n + 512)
                        pGa, tGa = nextp3()
                        unit(pGa[:, :], tGa, lambda kt: sGa[:, kt, :], lambda kt: hT[:, kt, cs], 8, [wGa])
                        S.add("act", lambda e, b=b, pGa=pGa: e.activation(out=tha[b][:], in_=pGa[:, :], func=AF.Tanh, scale=0.5),
                              reads=[tGa], writes=[t_tha[b]])
                        pA, tA = nextp3()
                        unit(pA[:, :], tA, lambda kt: sPa[:, kt, :], lambda kt: ya[:, kt, cs], 8, [wPa])
                        S.add("dve", lambda e, b=b, pA=pA: e.scalar_tensor_tensor(out=ma[b][:], in0=tha[b][:], scalar=1.0, in1=pA[:, :],
                                                                                  op0=ALU.add, op1=ALU.mult),
                              reads=[t_tha[b], tA], writes=[t_ma[b]])
                        pGb, tGb = nextp3()
                        unit(pGb[:, :], tGb, lambda kt: sGb[:, kt, :], lambda kt: hT[:, kt, cs], 8, [wGb])
                        S.add("act", lambda e, b=b, pGb=pGb: e.activation(out=tgb[b][:], in_=pGb[:, :], func=AF.Tanh, scale=0.5),
                              reads=[tGb], writes=[t_tgb[b]])
                        pBm, tBm = nextp3()
                        unit(pBm[:, :], tBm, lambda kt: sPb[:, kt, :], lambda kt: yb[:, kt, cs], 4, [wPb])
                        S.add("dve", lambda e, b=b, pBm=pBm: e.scalar_tensor_tensor(out=mb[b][:], in0=tgb[b][:], scalar=1.0, in1=pBm[:, :],
                                                                                    op0=ALU.add, op1=ALU.mult),
                              reads=[t_tgb[b], tBm], writes=[t_mb[b]])
                        S.add("pool", lambda e, b=b, ft=ft, cs=cs: e.tensor_tensor(out=mg[:, ft, cs], in0=ma[b][:], in1=mb[b][:], op=ALU.add),
                              reads=[t_ma[b], t_mb[b]], writes=[t_mg[half]])
                    if half == 1:
                        out_tile(ft)
            for i in range(8, 16):
                out_tile(i)
            S.emit(final=True)
    return nc


def _consts(core):
    q = core % 4
    hv = 1.0 if q > 0 else 0.0
    k = np.arange(128)[:, None]
    qq = np.arange(128)[None, :]
    D = (k <= qq).astype(np.float32)
    U = (k >= qq).astype(np.float32)
    hU = U * hv
    cst = np.zeros((128, NCST), np.float32)
    cst[:, 0:128] = np.eye(128, dtype=np.float32)
    bo = np.zeros((128, 128), np.float32)
    bo[0:64, 0:64] = 1.0 / 64
    bo[64:128, 64:128] = 1.0 / 64
    cst[:, 128:256] = bo
    cst[:, 256:320] = 2.0
    tiles = [np.concatenate([U, D, D, U], 1), np.concatenate([hU, D, D, U], 1), np.concatenate([D, U, D, U], 1),
             np.concatenate([U, D, U, D], 1), np.concatenate([hU, D, hU, D], 1)]
    for n in range(4):
        sl = slice(32 * n, 32 * n + 32)
        tiles.append(np.concatenate([hU[:, sl], D[:, sl]] * 8, 1))
    for i, t in enumerate(tiles):
        cst[:, 320 + 512 * i:320 + 512 * (i + 1)] = t
    return cst, hv


def _tile_w(w, nk):
    ncols = w.shape[1]
    return np.ascontiguousarray(w.reshape(nk, 128, ncols // 128, 128).transpose(2, 1, 0, 3))


_PROG = {}


def kernel(x, c, w_ada, b_ada, norm_w, w_in, conv_w, q_norm_w, k_norm_w, w_br_conv, w_br_attn, w_out):
    x = np.asarray(x, np.float32)
    c = np.asarray(c, np.float32)
    wada_t = _tile_w(np.asarray(w_ada, np.float32)[0], 8)
    win_t = _tile_w(np.asarray(w_in, np.float32)[0], 8)
    wbc_t = _tile_w(np.asarray(w_br_conv, np.float32)[0], 8)
    wba_t = _tile_w(np.asarray(w_br_attn, np.float32)[0], 4)
    wout = np.ascontiguousarray(np.asarray(w_out, np.float32)[0])
    rows = np.concatenate([np.asarray(b_ada, np.float32)[0], np.asarray(norm_w, np.float32)[0],
                           np.asarray(q_norm_w, np.float32)[0], np.asarray(k_norm_w, np.float32)[0]])[None, :]
    rows = np.ascontiguousarray(rows)
    cw = np.asarray(conv_w, np.float32)[0]
    in_maps = []
    for core in range(NCORES):
        b, q = core // 4, core % 4
        t0 = q * TOK
        cst, hv = _consts(core)
        xcat = np.zeros((2 * TOK, 1024), np.float32)
        if q > 0:
            xcat[0:TOK] = x[b, t0 - TOK:t0]
        xcat[TOK:] = x[b, t0:t0 + TOK]
        small = np.zeros((128, 32), np.float32)
        small[:, 0:24] = cw.reshape(3, 8, 128).transpose(2, 1, 0).reshape(128, 24)
        small[:, 24] = np.tile(np.asarray(q_norm_w, np.float32)[0], 2)
        small[:, 25] = np.tile(np.asarray(k_norm_w, np.float32)[0], 2)
        small[:, 26] = hv
        in_maps.append({
            "x": xcat.reshape(32, 128, 1024),
            "ccol": np.ascontiguousarray(c[b].reshape(8, 128).T),
            "cst": cst, "small": small, "rows": rows,
            "wada": wada_t, "win": win_t, "wbc": wbc_t, "wba": wba_t, "wout": wout,
        })
    if "nc" not in _PROG:
        _PROG["nc"] = build_program()
    res = run_bass_kernel_spmd(_PROG["nc"], in_maps[:NRUN], core_ids=list(range(NRUN)))
    if DEBUG:
        _PROG["res"] = res
    out = np.zeros((2, 4 * TOK, 1024), np.float32)
    for core in range(NRUN):
        b, q = core // 4, core % 4
        out[b, q * TOK:(q + 1) * TOK] = np.asarray(res.results[core]["out"]).reshape(TOK, 1024)
    return out
```

```python
import numpy as np
from contextlib import ExitStack
import concourse.bass as bass
import concourse.mybir as mybir
from concourse.bass_utils import run_bass_kernel_spmd

F32 = mybir.dt.float32
BF16 = mybir.dt.bfloat16
AF = mybir.ActivationFunctionType
ALU = mybir.AluOpType
AX = mybir.AxisListType

ENGS = ("pe", "act", "dve", "pool", "sp")
NCORES = 8
TOK = 2048
EPS = 1e-6
NCST = 320 + 9 * 512
M_T0, M_T0H, M_T1, M_T2, M_T2H, M_T3 = 0, 1, 2, 3, 4, 5
DEBUG = False
STOP = 9
SUB = 9
NRUN = NCORES


class Tok:
    __slots__ = ("name", "w", "r", "wd", "rd", "disjoint", "psum")

    def __init__(self, name="", disjoint=False, psum=False):
        self.name = name
        self.psum = psum
        self.w = {}
        self.r = {}
        self.wd = []
        self.rd = []
        self.disjoint = disjoint


class Op:
    __slots__ = ("eng", "fn", "deps", "dma", "semkey", "ticket", "sig")


class Sched:
    def __init__(self, nc, es, n_dma_sems=28):
        self.nc = nc
        self.sems = {}
        for e in ENGS:
            self.sems[e] = es.enter_context(nc.semaphore("s_" + e))
        self.ndma = n_dma_sems
        for i in range(n_dma_sems):
            self.sems[("d", i)] = es.enter_context(nc.semaphore("s_d%d" % i))
        self.semval = {k: 0 for k in self.sems}
        self.known = {e: {} for e in ENGS}
        self.dma_rr = 0
        self.dma_last = {}
        self.toks = []
        self.ops = {e: [] for e in ENGS}
        self.allops = []

    def tok(self, name="", disjoint=False, psum=False):
        t = Tok(name, disjoint, psum)
        self.toks.append(t)
        return t

    def add(self, eng, fn, reads=(), writes=(), dma=False):
        op = Op()
        op.eng = eng
        op.fn = fn
        op.dma = dma
        op.sig = False
        op.ticket = None
        deps = set()
        for t in reads:
            deps.update(t.w.values())
            deps.update(t.wd)
            if t.psum:
                deps.update(o for en, o in t.r.items() if en != eng)
        for t in writes:
            deps.update(t.r.values())
            deps.update(t.rd)
            if not t.disjoint:
                deps.update(t.w.values())
                deps.update(t.wd)
        if eng == "pe" and not dma:
            deps = {d for d in deps if not (d.eng == "pe" and not d.dma)}
        if dma:
            i = self.dma_rr
            self.dma_rr = (self.dma_rr + 1) % self.ndma
            op.semkey = ("d", i)
            prev = self.dma_last.get(i)
            if prev is not None:
                deps.add(prev)
            self.dma_last[i] = op
        else:
            op.semkey = eng
        deps.discard(op)
        op.deps = deps
        for d in deps:
            d.sig = True
        for t in reads:
            if dma:
                t.rd.append(op)
            else:
                t.r[eng] = op
        for t in writes:
            if not t.disjoint:
                t.w = {}
                t.wd = []
                t.r = {}
                t.rd = []
            if dma:
                t.wd.append(op)
            else:
                t.w[eng] = op
        self.ops[eng].append(op)
        self.allops.append(op)
        return op

    def wait_all(self, eng, ops):
        op = Op()
        op.eng = eng
        op.fn = None
        op.dma = False
        op.sig = False
        op.ticket = None
        op.semkey = eng
        op.deps = set(o for o in ops if o is not None)
        for d in op.deps:
            d.sig = True
        self.ops[eng].append(op)
        return op

    def emit(self, final=False):
        nc = self.nc
        for e in ENGS:
            pend = [o for o in self.ops[e] if o.dma]
            if pend:
                self.wait_all(e, pend)
        for op in self.allops:
            if op.sig and op.fn is not None:
                self.semval[op.semkey] += 16 if op.dma else 1
                op.ticket = self.semval[op.semkey]
        sched = self

        def run(ename, eng):
            known = sched.known[ename]
            for op in sched.ops[ename]:
                need = {}
                for d in op.deps:
                    assert d.ticket is not None, (ename, d.eng)
                    if need.get(d.semkey, 0) < d.ticket:
                        need[d.semkey] = d.ticket
                for k, v in need.items():
                    if known.get(k, 0) < v:
                        eng.wait_ge(sched.sems[k], v)
                        known[k] = v
                if op.fn is None:
                    continue
                ins = op.fn(eng)
                if op.sig:
                    ins.then_inc(sched.sems[op.semkey], 16 if op.dma else 1)

        with nc.Block(no_gpsimd_drain=True) as block:
            @block.tensor
            def _(e):
                run("pe", e)

            @block.scalar
            def _(e):
                run("act", e)

            @block.vector
            def _(e):
                run("dve", e)

            @block.gpsimd
            def _(e):
                run("pool", e)

            @block.sync
            def _(e):
                run("sp", e)
        for t in self.toks:
            t.w = {}
            t.r = {}
            t.wd = []
            t.rd = []
        self.toks = []
        self.ops = {e: [] for e in ENGS}
        self.allops = []


def build_program():
    nc = bass.Bass("TRN2", target_bir_lowering=False)
    dt = nc.dram_tensor
    x_d = dt("x", [32, 128, 1024], F32, kind="ExternalInput").ap()
    ccol_d = dt("ccol", [128, 8], F32, kind="ExternalInput").ap()
    cst_d = dt("cst", [128, NCST], F32, kind="ExternalInput").ap()
    small_d = dt("small", [128, 32], F32, kind="ExternalInput").ap()
    rows_d = dt("rows", [1, 4224], F32, kind="ExternalInput").ap()
    wada_d = dt("wada", [24, 128, 8, 128], F32, kind="ExternalInput").ap()
    win_d = dt("win", [88, 128, 8, 128], F32, kind="ExternalInput").ap()
    wbc_d = dt("wbc", [8, 128, 8, 128], F32, kind="ExternalInput").ap()
    wba_d = dt("wba", [8, 128, 4, 128], F32, kind="ExternalInput").ap()
    wout_d = dt("wout", [1024, 1024], F32, kind="ExternalInput").ap()
    out_d = dt("out", [16, 128, 1024], F32, kind="ExternalOutput").ap()
    if DEBUG:
        d_hT = dt("d_hT", [128, 8, TOK], BF16, kind="ExternalOutput").ap()
        d_hh = dt("d_hh", [128, 8, TOK], BF16, kind="ExternalOutput").ap()
        d_acsh = dt("d_acsh", [128, 16], F32, kind="ExternalOutput").ap()
        d_gate = dt("d_gate", [128, 1024], F32, kind="ExternalOutput").ap()
        d_negc = dt("d_negc", [128, 1], F32, kind="ExternalOutput").ap()
        d_yb = dt("d_yb", [128, 4, TOK], BF16, kind="ExternalOutput").ap()
        d_ya = dt("d_ya", [128, 8, TOK], BF16, kind="ExternalOutput").ap()
        d_mg = dt("d_mg", [128, 8, TOK], BF16, kind="ExternalOutput").ap()
        d_Q = [[dt("d_Q%d_%d" % (g, hd), [128, TOK], BF16, kind="ExternalOutput").ap() for hd in range(2)] for g in range(3)]
        d_K = [dt("d_K%d" % g, [128, (128, 512, 2048)[g] + TOK], BF16, kind="ExternalOutput").ap() for g in range(3)]
        d_V = [dt("d_V%d" % g, [128, (1, 4, 16)[g] + 16, 128], BF16, kind="ExternalOutput").ap() for g in range(3)]
        d_zs = dt("d_zs", [128, TOK], BF16, kind="ExternalOutput").ap()

    with ExitStack() as es:
        S = Sched(nc, es)

        def sbt(stack, name, shape, dtype):
            return stack.enter_context(nc.sbuf_tensor("sb_" + name, shape, dtype))

        def pst(stack, name, shape, dtype):
            return stack.enter_context(nc.psum_tensor("ps_" + name, shape, dtype))

        cst = sbt(es, "cst", [128, NCST], BF16)
        ident = cst[:, 0:128]
        bones = cst[:, 128:256]
        twos = cst[:, 256:320]

        def mask(i):
            return cst[:, 320 + 512 * i: 320 + 512 * (i + 1)]

        small = sbt(es, "small", [128, 32], F32)
        cwh = sbt(es, "cwh", [128, 24], F32)
        wqk = sbt(es, "wqk", [128, 1], F32)
        negc = sbt(es, "negc", [128, 1], F32)
        epsc = sbt(es, "epsc", [128, 1], F32)
        acsh = sbt(es, "acsh", [128, 16], F32)
        gate = sbt(es, "gate", [128, 1024], F32)
        hT = sbt(es, "hT", [128, 8, TOK], BF16)
        bufA = sbt(es, "bufA", [128, 8, TOK], BF16)
        hh2 = sbt(es, "hh2", [128, 8, 2], BF16)
        yb = sbt(es, "yb", [128, 4, TOK], BF16)
        NW = 10
        wring = [sbt(es, "wr%d" % i, [128, 8, 128], BF16) for i in range(NW)]
        wtok = [None] * NW
        wstate = {"i": 0}

        wpre = {}

        def wload(src, nk=8, key=None, prefetch=False):
            if key is not None and key in wpre:
                i = wpre.pop(key)
                wtok[i] = S.tok("w%d" % i)
                return wring[i], wtok[i]
            i = wstate["i"]
            wstate["i"] = (i + 1) % NW
            if wtok[i] is None or wtok[i] not in S.toks:
                wtok[i] = S.tok("w%d" % i)
            slot = wring[i]
            S.add("pool", lambda e: e.dma_start(out=slot[:, 0:nk, :], in_=src), writes=[wtok[i]], dma=True)
            if prefetch:
                wpre[key] = i
            return slot, wtok[i]

        def unit(ps_ap, ps_tok, lhs_fn, rhs_fn, nk, reads):
            for kt in range(nk):
                la, ra = lhs_fn(kt), rhs_fn(kt)
                S.add("pe", lambda e, kt=kt, la=la, ra=ra: e.matmul(ps_ap, lhsT=la, rhs=ra,
                                                                    start=(kt == 0), stop=(kt == nk - 1)),
                      reads=reads, writes=[ps_tok])

        with ExitStack() as p0:
            ccol = sbt(p0, "ccol", [128, 8], F32)
            th8 = sbt(p0, "th8", [128, 8], F32)
            scb = sbt(p0, "scb", [128, 8], BF16)
            modrow = sbt(p0, "modrow", [1, 3072], F32)
            nwrow = sbt(p0, "nwrow", [1, 1024], F32)
            arow = sbt(p0, "arow", [1, 1024], F32)
            qkrow = sbt(p0, "qkrow", [1, 128], F32)
            prow = sbt(p0, "prow", [1, 64], F32)
            c11 = sbt(p0, "c11", [1, 2], F32)
            onesf = sbt(p0, "onesf", [1, 128], F32)
            xs = [sbt(p0, "xs%d" % i, [128, 1024], F32) for i in range(4)]
            xh = [sbt(p0, "xh%d" % i, [128, 1024], BF16) for i in range(8)]
            junk = sbt(p0, "junk", [128, 1024], BF16)
            ss = sbt(p0, "ss", [128, 32], F32)
            vv = sbt(p0, "vv", [128, 32], F32)
            lv = sbt(p0, "lv", [128, 32], F32)
            rstd = sbt(p0, "rstd", [128, 32], F32)
            ps_row = pst(p0, "ps_row", [128, 512], F32)
            ps_col = pst(p0, "ps_col", [128, 512], F32)
            ps_g = [pst(p0, "ps_g%d" % i, [128, 512], F32) for i in range(2)]
            psT = [pst(p0, "psT%d" % i, [128, 1024], BF16) for i in range(4)]

            t_cst, t_small, t_ccol, t_modrow, t_nw, t_qk = (S.tok() for _ in range(6))
            t_cst.disjoint = True
            S.add("pool", lambda e: e.dma_start(out=cst[:, 0:320], in_=cst_d[:, 0:320]), writes=[t_cst], dma=True)
            S.add("sp", lambda e: e.dma_start(out=small[:], in_=small_d[:]), writes=[t_small], dma=True)
            S.add("sp", lambda e: e.dma_start(out=ccol[:], in_=ccol_d[:]), writes=[t_ccol], dma=True)
            S.add("sp", lambda e: e.dma_start(out=modrow[:], in_=rows_d[:, 0:3072]), writes=[t_modrow], dma=True)
            S.add("sp", lambda e: e.dma_start(out=nwrow[:], in_=rows_d[:, 3072:4096]), writes=[t_nw], dma=True)
            S.add("sp", lambda e: e.dma_start(out=qkrow[:], in_=rows_d[:, 4096:4224]), writes=[t_qk], dma=True)
            t_ones = S.tok()
            S.add("dve", lambda e: e.memset(onesf[:], 1.0), writes=[t_ones])
            S.add("dve", lambda e: e.memset(epsc[:], EPS))

            t_th8, t_scb = S.tok(), S.tok()
            S.add("act", lambda e: e.activation(out=th8[:], in_=ccol[:], func=AF.Tanh, scale=0.5),
                  reads=[t_ccol], writes=[t_th8])
            S.add("dve", lambda e: e.tensor_scalar(out=th8[:], in0=th8[:], scalar1=0.5, scalar2=0.5,
                                                   op0=ALU.mult, op1=ALU.add), reads=[t_th8], writes=[t_th8])
            S.add("dve", lambda e: e.tensor_tensor(out=scb[:], in0=th8[:], in1=ccol[:], op=ALU.mult),
                  reads=[t_th8, t_ccol], writes=[t_scb])

            t_cwh, t_wqk = S.tok(), S.tok()
            S.add("dve", lambda e: e.tensor_scalar(out=cwh[:], in0=small[:, 0:24], scalar1=0.5, scalar2=None,
                                                   op0=ALU.mult), reads=[t_small], writes=[t_cwh])
            S.add("dve", lambda e: e.scalar_tensor_tensor(out=wqk[:], in0=small[:, 24:25], scalar=0.125,
                                                          in1=small[:, 25:26], op0=ALU.mult, op1=ALU.mult),
                  reads=[t_small], writes=[t_wqk])
            t_prow, t_c11 = S.tok(), S.tok()
            S.add("dve", lambda e: e.tensor_tensor(out=prow[:], in0=qkrow[:, 0:64], in1=qkrow[:, 64:128], op=ALU.mult),
                  reads=[t_qk], writes=[t_prow])
            S.add("dve", lambda e: e.reduce_max(out=c11[:, 0:1], in_=prow[:], axis=AX.X, apply_absolute_value=True),
                  reads=[t_prow], writes=[t_c11])
            S.add("dve", lambda e: e.tensor_scalar(out=c11[:, 1:2], in0=c11[:, 0:1], scalar1=-8.0, scalar2=None,
                                                   op0=ALU.mult), reads=[t_c11], writes=[t_c11])

            t_xs = [S.tok() for _ in range(4)]
            t_xh = [S.tok() for _ in range(8)]
            t_junk = S.tok(disjoint=True)
            t_psT = [S.tok(psum=True) for _ in range(4)]
            t_h3 = S.tok(disjoint=True)

            def front(gi):
                for t in range(4):
                    i = 4 * gi + t
                    xb = xs[t]
                    hb = xh[4 * (gi % 2) + t]
                    t_stat = S.tok()
                    S.add("sp", lambda e, i=i, xb=xb: e.dma_start(out=xb[:], in_=x_d[i]), writes=[t_xs[t]], dma=True)
                    S.add("act", lambda e, i=i, xb=xb: e.activation(out=junk[:], in_=xb[:], func=AF.Square,
                                                                    accum_out=ss[:, i:i + 1]),
                          reads=[t_xs[t]], writes=[t_junk, t_stat])
                    S.add("dve", lambda e, i=i: e.tensor_scalar(out=vv[:, i:i + 1], in0=ss[:, i:i + 1], scalar1=1.0 / 1024,
                                                                scalar2=EPS, op0=ALU.mult, op1=ALU.add),
                          reads=[t_stat], writes=[t_stat])
                    S.add("act", lambda e, i=i: e.activation(out=lv[:, i:i + 1], in_=vv[:, i:i + 1], func=AF.Ln),
                          reads=[t_stat], writes=[t_stat])
                    S.add("act", lambda e, i=i: e.activation(out=rstd[:, i:i + 1], in_=lv[:, i:i + 1], func=AF.Exp, scale=-0.5),
                          reads=[t_stat], writes=[t_stat])
                    S.add("dve", lambda e, i=i, xb=xb, hb=hb: e.tensor_scalar(out=hb[:], in0=xb[:], scalar1=rstd[:, i:i + 1],
                                                                              scalar2=None, op0=ALU.mult),
                          reads=[t_xs[t], t_stat], writes=[t_xh[4 * (gi % 2) + t]])

            def back(gi):
                dstbuf = bufA if gi < 4 else hT
                c0 = (gi % 4) * 512
                wr = [t_h3] if gi == 3 else []
                for j in range(4):
                    for kt in (2 * j, 2 * j + 1):
                        for t in range(4):
                            hi = 4 * (gi % 2) + t
                            S.add("pe", lambda e, j=j, kt=kt, t=t, hi=hi: e.transpose(
                                psT[j][:, (kt % 2) * 512 + t * 128:(kt % 2) * 512 + (t + 1) * 128],
                                xh[hi][:, kt * 128:(kt + 1) * 128], ident),
                                reads=[t_xh[hi], t_cst], writes=[t_psT[j]])
                    for kt in (2 * j, 2 * j + 1):
                        dst = dstbuf[:, kt, c0:c0 + 512]
                        src = psT[j][:, (kt % 2) * 512:(kt % 2) * 512 + 512]
                        if j % 2 == 0:
                            S.add("dve", lambda e, dst=dst, src=src, kt=kt: e.tensor_scalar(
                                out=dst, in0=src, scalar1=acsh[:, kt:kt + 1], scalar2=acsh[:, 8 + kt:9 + kt],
                                op0=ALU.mult, op1=ALU.add), reads=[t_psT[j], t_acsh], writes=wr)
                        else:
                            S.add("act", lambda e, dst=dst, src=src, kt=kt: e.activation(
                                out=dst, in_=src, func=AF.Identity, scale=acsh[:, kt:kt + 1], bias=acsh[:, 8 + kt:9 + kt]),
                                reads=[t_psT[j], t_acsh], writes=wr)
                if gi == 3:
                    S.add("dve", lambda e: e.tensor_copy(out=hh2[:], in_=bufA[:, :, TOK - 2:TOK]), reads=[t_h3])

            t_acsh = S.tok()
            front(0)
            front(1)

            t_psrow = S.tok(psum=True)

            def mod_chunk(ci):
                for j in range(4):
                    ct = 4 * ci + j
                    slot, wt = wload(wada_d[ct])
                    unit(ps_row[0:1, j * 128:(j + 1) * 128], t_psrow,
                         lambda kt: scb[:, kt:kt + 1], lambda kt, slot=slot: slot[:, kt, :], 8, [wt, t_scb])
                c0 = ci * 512
                S.add("dve", lambda e, c0=c0: e.tensor_tensor(out=modrow[:, c0:c0 + 512], in0=modrow[:, c0:c0 + 512],
                                                              in1=ps_row[0:1, :], op=ALU.add),
                      reads=[t_psrow, t_modrow], writes=[t_modrow])

            for ci in (2, 3, 0, 1):
                mod_chunk(ci)
            t_arow = S.tok()
            S.add("dve", lambda e: e.scalar_tensor_tensor(out=arow[:], in0=modrow[:, 1024:2048], scalar=1.0, in1=nwrow[:],
                                                          op0=ALU.add, op1=ALU.mult),
                  reads=[t_modrow, t_nw], writes=[t_arow])
            t_pscol, t_negc = S.tok(psum=True), S.tok()
            for kt in range(8):
                S.add("pe", lambda e, kt=kt: e.matmul(ps_col[:, kt:kt + 1], lhsT=arow[0:1, kt * 128:(kt + 1) * 128],
                                                      rhs=onesf[0:1, 0:1], start=True, stop=True),
                      reads=[t_arow, t_ones], writes=[t_pscol])
                S.add("pe", lambda e, kt=kt: e.matmul(ps_col[:, 8 + kt:9 + kt], lhsT=modrow[0:1, kt * 128:(kt + 1) * 128],
                                                      rhs=onesf[0:1, 0:1], start=True, stop=True),
                      reads=[t_modrow, t_ones], writes=[t_pscol])
            S.add("pe", lambda e: e.matmul(ps_col[:, 16:17], lhsT=onesf[0:1, 0:128], rhs=c11[0:1, 1:2],
                                           start=True, stop=True), reads=[t_c11, t_ones], writes=[t_pscol])
            S.add("dve", lambda e: e.tensor_copy(out=acsh[:], in_=ps_col[:, 0:16]), reads=[t_pscol], writes=[t_acsh])
            S.add("dve", lambda e: e.tensor_copy(out=negc[:], in_=ps_col[:, 16:17]), reads=[t_pscol], writes=[t_negc])

            for gi in range(8):
                back(gi)
                if gi + 2 < 8:
                    front(gi + 2)

            for ci in (4, 5):
                mod_chunk(ci)
            t_psg, t_gate = S.tok(psum=True), S.tok(disjoint=True)
            for h in range(2):
                S.add("pe", lambda e, h=h: e.matmul(ps_g[h][:, :], lhsT=onesf[0:1, 0:128],
                                                    rhs=modrow[0:1, 2048 + 512 * h:2560 + 512 * h], start=True, stop=True),
                      reads=[t_modrow, t_ones], writes=[t_psg])
                S.add("act", lambda e, h=h: e.activation(out=gate[:, 512 * h:512 * h + 512], in_=ps_g[h][:, :],
                                                         func=AF.Copy, scale=0.5), reads=[t_psg], writes=[t_gate])
            for ti in (44, 32, 56, 48):
                wload(win_d[ti], key=("win", ti), prefetch=True)
            S.emit()
        def dump0():
            for dst, src in ((d_hT, hT), (d_hh, bufA), (d_acsh, acsh), (d_gate, gate), (d_negc, negc)):
                S.add("sp", lambda e, dst=dst, src=src: e.dma_start(out=dst[:], in_=src[:]), dma=True)

        if STOP <= 0:
            if DEBUG:
                dump0()
                S.emit()
            return nc

        HG = (128, 512, 2048)
        DIL = (1, 4, 16)
        with ExitStack() as p1:
            if DEBUG:
                dump0()
            Qn = [sbt(p1, "Qn%d" % g, [128, TOK], BF16) for g in range(3)]
            print("p1 sbuf remaining before rest", nc.sbuf_bytes_remaining)
            Kn = [sbt(p1, "Kn%d" % g, [128, HG[g] + TOK], BF16) for g in range(3)]
            Vt = [sbt(p1, "Vt%d" % g, [128, HG[g] // 128 + 16, 128], BF16) for g in range(3)]
            sq = [sbt(p1, "sq%d" % i, [128, 512], BF16) for i in range(2)]
            lnv = [sbt(p1, "lnv%d" % i, [128, 512], F32) for i in range(2)]
            NE = 8
            Eb = [sbt(p1, "Eb%d" % i, [128, 512], BF16) for i in range(NE)]
            Pb = [sbt(p1, "Pb%d" % i, [128, 512], BF16) for i in range(NE)]
            zs = sbt(p1, "zs", [128, TOK], BF16)
            thz = [sbt(p1, "thz%d" % i, [128, 512], F32) for i in range(2)]
            rec = [sbt(p1, "rec%d" % i, [128, 512], F32) for i in range(2)]
            ton = [sbt(p1, "ton%d" % i, [128, 512], F32) for i in range(2)]
            bk = [pst(p1, "bk%d" % i, [128, 512], F32) for i in range(8)]
            tbk = [S.tok(psum=True) for _ in range(8)]
            NPP = 5
            pp, t_pp = bk[0:5], tbk[0:5]
            pq, t_pq = bk[5:7], tbk[5:7]
            NS = 4
            pS, t_pS = bk[0:4], tbk[0:4]
            pOs, t_pOs = [bk[4], bk[6]], [tbk[4], tbk[6]]
            pMs, t_pMs = [bk[5], bk[7]], [tbk[5], tbk[7]]

            t_sq = [S.tok(), S.tok()]
            t_lnv = [S.tok(), S.tok()]
            t_Q = [S.tok(disjoint=True) for _ in range(3)]
            t_K = [S.tok(disjoint=True) for _ in range(3)]
            t_V = [S.tok(disjoint=True) for _ in range(3)]
            t_E = [S.tok() for _ in range(NE)]
            t_P = [S.tok() for _ in range(NE)]
            t_zs = S.tok(disjoint=True)
            t_thz = [S.tok(), S.tok()]
            t_rec = [S.tok(), S.tok()]
            t_ton = [S.tok(), S.tok()]
            t_yb = S.tok(disjoint=True)
            ucnt = {"u": 0}

            def qk_unit(slot, wt, src_fn, N, dst_ap, dst_tok, scalar, r, dst_ap2=None):
                u = ucnt["u"]
                ucnt["u"] += 1
                pb = u % NPP
                qc = ucnt.get("q", 0)
                ucnt["q"] = qc + 1
                b = qc % 2
                ps = pp[pb][:, 0:N]
                unit(ps, t_pp[pb], lambda kt: slot[:, kt, :], src_fn, 8, [wt])
                S.add("act", lambda e: e.activation(out=sq[b][:, 0:N], in_=ps, func=AF.Square),
                      reads=[t_pp[pb]], writes=[t_sq[b]])
                return lambda: qk_part2(ps, pb, b, N, dst_ap, dst_tok, scalar, r, dst_ap2)

            def qk_part2(ps, pb, b, N, dst_ap, dst_tok, scalar, r, dst_ap2):
                S.add("pe", lambda e: e.matmul(pq[b][:, 0:N], lhsT=bones, rhs=sq[b][:, 0:N], start=True, stop=True),
                      reads=[t_sq[b]], writes=[t_pq[b]])
                S.add("act", lambda e: e.activation(out=lnv[b][:, 0:N], in_=pq[b][:, 0:N], func=AF.Ln, bias=EPS),
                      reads=[t_pq[b]], writes=[t_lnv[b]])
                S.add("act", lambda e: e.activation(out=lnv[b][:, 0:N], in_=lnv[b][:, 0:N], func=AF.Exp, scale=-0.5),
                      reads=[t_lnv[b]], writes=[t_lnv[b]])
                if dst_ap2 is None:
                    S.add("dve", lambda e: e.scalar_tensor_tensor(out=dst_ap, in0=ps_view(ps, r), scalar=scalar,
                                                                  in1=ps_view(lnv[b][:, 0:N], r),
                                                                  op0=ALU.mult, op1=ALU.mult),
                          reads=[t_pp[pb], t_lnv[b]], writes=[dst_tok])
                else:
                    for hd, d_ap in enumerate((dst_ap, dst_ap2)):
                        rows = slice(64 * hd, 64 * hd + 64)
                        S.add("dve", lambda e, rows=rows, d_ap=d_ap: e.scalar_tensor_tensor(
                            out=d_ap, in0=ps_view(pp[pb][rows, 0:N], r), scalar=scalar[rows, :],
                            in1=ps_view(lnv[b][rows, 0:N], r), op0=ALU.mult, op1=ALU.mult),
                            reads=[t_pp[pb], t_lnv[b]], writes=[dst_tok])

            view_state = {}

            def ps_view(src, r):
                if r == 1:
                    return src
                return src.rearrange("p (a r) -> p r a", r=r)

            def dst_view(buf, base, Lsub, r, a0, na):
                if r == 1:
                    return buf[:, base + a0: base + a0 + na]
                return buf[:, base:base + r * Lsub].rearrange("p (r l) -> p r l", r=r)[:, :, a0:a0 + na]

            def load_pair(m):
                W = {}
                for g in range(3):
                    W[("k", g)] = wload(win_d[44 + 4 * g + m], key=("win", 44 + 4 * g + m))
                    W[("q", g)] = wload(win_d[32 + 4 * g + m], key=("win", 32 + 4 * g + m))
                    W[("v", g)] = wload(win_d[56 + 4 * g + m], key=("win", 56 + 4 * g + m))
                W["z"] = wload(win_d[68 + m])
                return W

            Wnext = load_pair(0)
            t_msk = S.tok(disjoint=True)
            for c0 in range(320, NCST, 1152):
                S.add("pool", lambda e, c0=c0: e.dma_start(out=cst[:, c0:c0 + 1152], in_=cst_d[:, c0:c0 + 1152]),
                      writes=[t_msk], dma=True)
            for m in range(4):
                W = Wnext
                qk_list, v_list, z_list = [], [], []
                for g in range(3):
                    r = DIL[g]
                    H = HG[g]
                    L = TOK // r
                    slotk, wtk = W[("k", g)]
                    nh = max(1, H // 512)
                    for u in range(nh):
                        N = min(512, H)
                        c0 = TOK - H + 512 * u
                        na = N // r
                        dst = dst_view(Kn[g], 0, 128, r, (512 * u) // r, na)
                        qk_list.append(lambda slotk=slotk, wtk=wtk, c0=c0, N=N, dst=dst, g=g, r=r: qk_unit(
                            slotk, wtk, lambda kt: bufA[:, kt, c0:c0 + N], N, dst, t_K[g], 1.0, r))
                    for n in range(4):
                        dst = dst_view(Kn[g], H, L, r, (512 * n) // r, 512 // r)
                        qk_list.append(lambda slotk=slotk, wtk=wtk, n=n, dst=dst, g=g, r=r: qk_unit(
                            slotk, wtk, lambda kt: hT[:, kt, 512 * n:512 * n + 512], 512, dst, t_K[g], 1.0, r))
                    slotq, wtq = W[("q", g)]
                    for n in range(4):
                        dstq = dst_view(Qn[g], 0, L, r, (512 * n) // r, 512 // r)
                        qk_list.append(lambda slotq=slotq, wtq=wtq, n=n, dstq=dstq, g=g, r=r: qk_unit(
                            slotq, wtq, lambda kt: hT[:, kt, 512 * n:512 * n + 512], 512, dstq, t_Q[g], wqk[:, 0:1], r))
                    slotv, wtv = W[("v", g)]
                    nhb = H // 128
                    nkb = nhb + 16

                    def v_group(kb0, g=g, r=r, H=H, L=L, slotv=slotv, wtv=wtv, nhb=nhb, nkb=nkb):
                        u = ucnt["u"]
                        ucnt["u"] += 1
                        b = u % NPP
                        nb = min(4, nkb - kb0)
                        for j in range(nb):
                            kb = kb0 + j
                            if kb < nhb:
                                srcbuf, start = bufA, TOK - H + kb
                            else:
                                o = kb - nhb
                                rr, bb = o // (L // 128), o % (L // 128)
                                srcbuf, start = hT, 128 * bb * r + rr
                            if r == 1:
                                lhs_fn = lambda kt, srcbuf=srcbuf, start=start: srcbuf[:, kt, start:start + 128]
                            else:
                                lhs_fn = lambda kt, srcbuf=srcbuf, start=start, r=r: \
                                    srcbuf[:, kt, start - (start % r):start - (start % r) + 128 * r].rearrange(
                                        "p (a r) -> p r a", r=r)[:, start % r, :]
                            unit(pp[b][:, j * 128:(j + 1) * 128], t_pp[b], lhs_fn,
                                 lambda kt: slotv[:, kt, :], 8, [wtv])
                        S.add("dve", lambda e, b=b, nb=nb, g=g, kb0=kb0: e.tensor_copy(
                            out=Vt[g][:, kb0:kb0 + nb, :].rearrange("p a b -> p (a b)"), in_=pp[b][:, 0:nb * 128]),
                            reads=[t_pp[b]], writes=[t_V[g]])

                    for kb0 in range(0, nkb, 4):
                        v_list.append(lambda kb0=kb0, v_group=v_group: v_group(kb0))
                slotz, wtz = W["z"]

                def z_unit(n, slotz=slotz, wtz=wtz):
                    u = ucnt["u"]
                    ucnt["u"] += 1
                    b = u % NPP
                    zb = n % 2
                    unit(pp[b][:, :], t_pp[b], lambda kt: slotz[:, kt, :],
                         lambda kt: hT[:, kt, 512 * n:512 * n + 512], 8, [wtz])
                    S.add("act", lambda e, b=b, zb=zb: e.activation(out=thz[zb][:], in_=pp[b][:, :], func=AF.Tanh, scale=0.5),
                          reads=[t_pp[b]], writes=[t_thz[zb]])
                    S.add("dve", lambda e, b=b, n=n, zb=zb: e.scalar_tensor_tensor(
                        out=zs[:, 512 * n:512 * n + 512], in0=thz[zb][:], scalar=1.0, in1=pp[b][:, :],
                        op0=ALU.add, op1=ALU.mult), reads=[t_thz[zb], t_pp[b]], writes=[t_zs])

                for n in range(4):
                    z_list.append(lambda n=n, z_unit=z_unit: z_unit(n))
                light = v_list
                pend = None
                while qk_list or light:
                    cont = qk_list.pop(0)() if qk_list else None
                    if light:
                        light.pop(0)()
                        if pend is not None:
                            pend()
                            pend = None
                        if cont is not None:
                            cont()
                    else:
                        if pend is not None:
                            pend()
                        pend = cont
                if pend is not None:
                    pend()
                for zf in z_list:
                    zf()
                if m + 1 < 4:
                    Wnext = load_pair(m + 1)
                else:
                    for ti in (8, 16, 24, 0):
                        wload(win_d[ti], key=("win", ti), prefetch=True)

                def batches(n):
                    res = []
                    i0 = 4 * n
                    if n == 0:
                        prev = (0, 0, i0 * 128, 128, (0, 1))
                    else:
                        prev = (0, 128 + (i0 - 1) * 128, i0 * 128, 128, (0, 1))
                    last = (0, 128 + (i0 + 3) * 128, (i0 + 3) * 128, 128, (384, 1))
                    res.append((M_T0H if n == 0 else M_T0, [prev, last, (0, 128 + i0 * 128, i0 * 128, 256, (0, 1))]))
                    res.append((M_T1, [(0, 128 + (i0 + j) * 128, (i0 + j) * 128, 256, (j * 128, 1)) for j in (1, 2)]))
                    for rp in range(2):
                        pcs = []
                        for rr in (2 * rp, 2 * rp + 1):
                            if n == 0:
                                pcs.append((1, rr * 128, rr * 512, 128, (rr, 4)))
                            else:
                                pcs.append((1, 512 + rr * 512 + (n - 1) * 128, rr * 512 + n * 128, 128, (rr, 4)))
                            pcs.append((1, 512 + rr * 512 + n * 128, rr * 512 + n * 128, 128, (rr, 4)))
                        res.append((M_T2H if n == 0 else M_T2, pcs))
                    for rb in range(2):
                        pcs = []
                        for rr in range(8 * rb, 8 * rb + 8):
                            pcs.append((2, rr * 128, rr * 128 + 32 * n, 32, (rr, 16)))
                            pcs.append((2, 2048 + rr * 128, rr * 128 + 32 * n, 32, (rr, 16)))
                        res.append((M_T3 + n, pcs))
                    return res

                blist = []
                for n in range(4):
                    bl = batches(n)
                    for bi, (mi, pcs) in enumerate(bl):
                        blist.append((n, bi == 0, bi == len(bl) - 1, mi, pcs))
                NB = len(blist)

                def emit_S(bidx):
                    n, first, lastb, mi, pcs = blist[bidx]
                    off = 0
                    for (g, kcol, qcol, N, ospec) in pcs:
                        for hd in range(2):
                            sb3 = (2 * bidx + hd) % NS
                            rows = slice(64 * hd, 64 * hd + 64)
                            S.add("pe", lambda e, g=g, kcol=kcol, qcol=qcol, N=N, rows=rows, off=off, sb3=sb3: e.matmul(
                                pS[sb3][:, off:off + N], lhsT=Kn[g][rows, kcol:kcol + 128],
                                rhs=Qn[g][rows, qcol:qcol + N], start=True, stop=True),
                                reads=[t_K[g], t_Q[g]], writes=[t_pS[sb3]])
                        off += N
                    assert off == 512
                    for hd in range(2):
                        sb3 = (2 * bidx + hd) % NS
                        eb = (2 * bidx + hd) % NE
                        S.add("act", lambda e, sb3=sb3, eb=eb: e.activation(out=Eb[eb][:], in_=pS[sb3][:, :], func=AF.Exp,
                                                                            bias=negc[:, 0:1], scale=1.0),
                              reads=[t_pS[sb3]], writes=[t_E[eb]])
                        S.add("dve", lambda e, eb=eb, mi=mi: e.tensor_tensor(out=Pb[eb][:], in0=Eb[eb][:], in1=mask(mi),
                                                                             op=ALU.mult),
                              reads=[t_E[eb], t_msk], writes=[t_P[eb]])

                def emit_PV(bidx):
                    n, first, lastb, mi, pcs = blist[bidx]
                    pO, t_pO, pM, t_pM = pOs[n % 2], t_pOs[n % 2], pMs[n % 2], t_pMs[n % 2]
                    np_ = len(pcs)
                    for hd in range(2):
                        eb = (2 * bidx + hd) % NE
                        off = 0
                        for pi, (g, kcol, qcol, N, (ostart, ostep)) in enumerate(pcs):
                            vkb = kcol // 128
                            st = first and pi == 0
                            sp_ = lastb and pi == np_ - 1
                            rows = slice(64 * hd, 64 * hd + 64)

                            def oview(t, rows=rows, ostart=ostart, ostep=ostep, N=N):
                                if ostep == 1:
                                    return t[rows, ostart:ostart + N]
                                return t[rows, :].rearrange("p (a r) -> p r a", r=ostep)[:, ostart, 0:N]

                            S.add("pe", lambda e, g=g, vkb=vkb, hd=hd, off=off, N=N, eb=eb, st=st, sp_=sp_, oview=oview:
                                  e.matmul(oview(pO), lhsT=Vt[g][:, vkb, 64 * hd:64 * hd + 64],
                                           rhs=Pb[eb][:, off:off + N], start=st, stop=sp_, skip_group_check=True),
                                  reads=[t_P[eb], t_V[g]], writes=[t_pO])
                            S.add("pe", lambda e, hd=hd, off=off, N=N, eb=eb, st=st, sp_=sp_, oview=oview:
                                  e.matmul(oview(pM), lhsT=twos, rhs=Pb[eb][:, off:off + N], start=st, stop=sp_,
                                           skip_group_check=True),
                                  reads=[t_P[eb]], writes=[t_pM])
                            off += N
                    if lastb:
                        b = n % 2
                        S.add("act", lambda e, b=b, pM=pM: e.activation(out=rec[b][:], in_=pM[:, :], func=AF.Ln),
                              reads=[t_pM], writes=[t_rec[b]])
                        S.add("act", lambda e, b=b: e.activation(out=rec[b][:], in_=rec[b][:], func=AF.Exp, scale=-1.0),
                              reads=[t_rec[b]], writes=[t_rec[b]])
                        S.add("dve", lambda e, b=b, pO=pO: e.tensor_tensor(out=ton[b][:], in0=pO[:, :], in1=rec[b][:], op=ALU.mult),
                              reads=[t_pO, t_rec[b]], writes=[t_ton[b]])
                        S.add("pool", lambda e, b=b, n=n, m=m: e.tensor_tensor(
                            out=yb[:, m, 512 * n:512 * n + 512], in0=ton[b][:], in1=zs[:, 512 * n:512 * n + 512],
                            op=ALU.mult), reads=[t_ton[b], t_zs], writes=[t_yb])

                emit_S(0)
                emit_S(1)
                emit_S(2)
                for bidx in range(NB):
                    if bidx + 3 < NB:
                        emit_S(bidx + 3)
                    emit_PV(bidx)
            if DEBUG and STOP <= 1:
                S.emit()
                for g in range(3):
                    for hd in range(2):
                        S.add("sp", lambda e, g=g, hd=hd: e.dma_start(out=d_Q[g][hd][:], in_=Qn[g][hd][:]), dma=True)
                    S.add("sp", lambda e, g=g: e.dma_start(out=d_K[g][:], in_=Kn[g][:]), dma=True)
                    S.add("sp", lambda e, g=g: e.dma_start(out=d_V[g][:], in_=Vt[g][:]), dma=True)
                S.add("sp", lambda e: e.dma_start(out=d_zs[:], in_=zs[:]), dma=True)
            S.emit()
        if STOP <= 1:
            if DEBUG:
                S.add("sp", lambda e: e.dma_start(out=d_yb[:], in_=yb[:]), dma=True)
                S.emit()
            return nc

        ya = bufA
        with ExitStack() as p2:
            if DEBUG:
                S.add("sp", lambda e: e.dma_start(out=d_yb[:], in_=yb[:]), dma=True)
            csb = [sbt(p2, "csb%d" % i, [128, 512], F32) for i in range(2)]
            ub = [sbt(p2, "ub%d" % i, [128, 514], F32) for i in range(2)]
            tb = [sbt(p2, "tb%d" % i, [128, 512], F32) for i in range(2)]
            thb = [sbt(p2, "thb%d" % i, [128, 512], F32) for i in range(2)]
            szb = [sbt(p2, "szb%d" % i, [128, 512], F32) for i in range(2)]
            bzb = [sbt(p2, "bzb%d" % i, [128, 512], F32) for i in range(2)]
            ch2 = sbt(p2, "ch2", [128, 2], F32)
            pc = [pst(p2, "pc%d" % i, [128, 512], F32) for i in range(6)]
            ph = pst(p2, "ph", [128, 512], F32)
            t_pc = [S.tok(psum=True) for _ in range(6)]
            t_ph = S.tok(psum=True)
            t_csb = [S.tok(), S.tok()]
            t_ub = [S.tok(), S.tok()]
            t_tb = [S.tok(), S.tok()]
            t_thb = [S.tok(), S.tok()]
            t_szb = [S.tok(), S.tok()]
            t_bzb = [S.tok(), S.tok()]
            t_ch2 = S.tok()
            t_ya = S.tok(disjoint=True)
            pcn = {"i": 0}

            def nextp():
                i = pcn["i"]
                pcn["i"] = (i + 1) % 6
                return pc[i], t_pc[i]

            it = 0

            def load_ct(ct):
                return [wload(win_d[t], key=("win", t)) for t in (8 + ct, 16 + ct, 24 + ct, ct)]

            Wn = load_ct(0)
            for ct in range(8):
                (sC, wC), (sX, wX), (sZ, wZ), (sB, wB) = Wn
                if ct + 1 < 8:
                    Wn = load_ct(ct + 1)
                if ct == 7:
                    wload(win_d[72], key=("win", 72), prefetch=True)
                    wload(wbc_d[0], key=("wbc", 0), prefetch=True)
                    wload(win_d[80], key=("win", 80), prefetch=True)
                    wload(wba_d[0], nk=4, key=("wba", 0), prefetch=True)
                unit(ph[:, 0:2], t_ph, lambda kt: sC[:, kt, :], lambda kt: hh2[:, kt, :], 8, [wC])
                unit(ph[:, 2:4], t_ph, lambda kt: sX[:, kt, :], lambda kt: hh2[:, kt, :], 8, [wX])
                S.add("act", lambda e: e.activation(out=ch2[:], in_=ph[:, 0:2], func=AF.Copy, scale=small[:, 26:27]),
                      reads=[t_ph], writes=[t_ch2])
                for n in range(4):
                    b = it % 2
                    it += 1
                    rhs_fn = lambda kt, n=n: hT[:, kt, 512 * n:512 * n + 512]
                    pC, tC = nextp()
                    unit(pC[:, :], tC, lambda kt: sC[:, kt, :], rhs_fn, 8, [wC])
                    S.add("act", lambda e, b=b, pC=pC: e.activation(out=csb[b][:], in_=pC[:, :], func=AF.Copy),
                          reads=[tC], writes=[t_csb[b]])
                    pX, tX = nextp()
                    unit(pX[:, :], tX, lambda kt: sX[:, kt, :], rhs_fn, 8, [wX])
                    if n == 0:
                        S.add("dve", lambda e, b=b: e.tensor_tensor(out=ub[b][:, 0:2], in0=ph[:, 2:4], in1=ch2[:], op=ALU.mult),
                              reads=[t_ph, t_ch2], writes=[t_ub[b]])
                    else:
                        S.add("dve", lambda e, b=b: e.tensor_copy(out=ub[b][:, 0:2], in_=ub[1 - b][:, 512:514]),
                              reads=[t_ub[1 - b]], writes=[t_ub[b]])
                    S.add("dve", lambda e, b=b, pX=pX: e.tensor_tensor(out=ub[b][:, 2:514], in0=pX[:, :], in1=csb[b][:], op=ALU.mult),
                          reads=[tX, t_csb[b], t_ub[b]], writes=[t_ub[b]])
                    S.add("dve", lambda e, b=b, ct=ct: e.tensor_scalar(out=tb[b][:], in0=ub[b][:, 2:514],
                                                                       scalar1=cwh[:, 3 * ct + 2:3 * ct + 3], scalar2=None,
                                                                       op0=ALU.mult), reads=[t_ub[b]], writes=[t_tb[b]])
                    S.add("dve", lambda e, b=b, ct=ct: e.scalar_tensor_tensor(out=tb[b][:], in0=ub[b][:, 1:513],
                                                                              scalar=cwh[:, 3 * ct + 1:3 * ct + 2], in1=tb[b][:],
                                                                              op0=ALU.mult, op1=ALU.add),
                          reads=[t_ub[b], t_tb[b]], writes=[t_tb[b]])
                    S.add("dve", lambda e, b=b, ct=ct: e.scalar_tensor_tensor(out=tb[b][:], in0=ub[b][:, 0:512],
                                                                              scalar=cwh[:, 3 * ct:3 * ct + 1], in1=tb[b][:],
                                                                              op0=ALU.mult, op1=ALU.add),
                          reads=[t_ub[b], t_tb[b]], writes=[t_tb[b]])
                    pZ, tZ = nextp()
                    unit(pZ[:, :], tZ, lambda kt: sZ[:, kt, :], rhs_fn, 8, [wZ])
                    S.add("act", lambda e, b=b, pZ=pZ: e.activation(out=thb[b][:], in_=pZ[:, :], func=AF.Tanh, scale=0.5),
                          reads=[tZ], writes=[t_thb[b]])
                    S.add("dve", lambda e, b=b, pZ=pZ: e.scalar_tensor_tensor(out=szb[b][:], in0=thb[b][:], scalar=1.0, in1=pZ[:, :],
                                                                              op0=ALU.add, op1=ALU.mult),
                          reads=[t_thb[b], tZ], writes=[t_szb[b]])
                    pB, tB = nextp()
                    unit(pB[:, :], tB, lambda kt: sB[:, kt, :], rhs_fn, 8, [wB])
                    S.add("dve", lambda e, b=b, pB=pB: e.tensor_tensor(out=bzb[b][:], in0=pB[:, :], in1=szb[b][:], op=ALU.mult),
                          reads=[tB, t_szb[b]], writes=[t_bzb[b]])
                    S.add("pool", lambda e, b=b, ct=ct, n=n: e.tensor_tensor(out=ya[:, ct, 512 * n:512 * n + 512], in0=bzb[b][:],
                                                                            in1=tb[b][:], op=ALU.mult),
                          reads=[t_bzb[b], t_tb[b]], writes=[t_ya])
            S.emit()
        if STOP <= 2:
            if DEBUG:
                S.add("sp", lambda e: e.dma_start(out=d_yb[:], in_=yb[:]), dma=True)
                S.add("sp", lambda e: e.dma_start(out=d_ya[:], in_=bufA[:]), dma=True)
                S.emit()
            return nc

        with ExitStack() as p34:
            mg = sbt(p34, "mg", [128, 8, TOK], BF16)
            wo = sbt(p34, "wo", [128, 8, 1024], BF16)
            xr = [sbt(p34, "xr%d" % i, [128, 1024], F32) for i in range(3)]
            tha = [sbt(p34, "tha%d" % i, [128, 512], F32) for i in range(2)]
            tgb = [sbt(p34, "tgb%d" % i, [128, 512], F32) for i in range(2)]
            ma = [sbt(p34, "ma%d" % i, [128, 512], F32) for i in range(2)]
            mb = [sbt(p34, "mb%d" % i, [128, 512], F32) for i in range(2)]
            tg = [sbt(p34, "tg%d" % i, [128, 1024], F32) for i in range(2)]
            ob = [sbt(p34, "ob%d" % i, [128, 1024], F32) for i in range(2)]
            pc = [pst(p34, "pd%d" % i, [128, 512], F32) for i in range(6)]
            po = [pst(p34, "po%d" % i, [128, 512], F32) for i in range(2)]
            if DEBUG:
                S.add("sp", lambda e: e.dma_start(out=d_ya[:], in_=ya[:]), dma=True)
            t_pc = [S.tok(psum=True) for _ in range(6)]
            t_po = [S.tok(psum=True) for _ in range(2)]
            t_tha = [S.tok(), S.tok()]
            t_tgb = [S.tok(), S.tok()]
            t_ma = [S.tok(), S.tok()]
            t_mb = [S.tok(), S.tok()]
            t_mg = [S.tok(disjoint=True), S.tok(disjoint=True)]
            t_wo = S.tok(disjoint=True)
            t_xr = [S.tok() for _ in range(3)]
            t_tg = [S.tok(disjoint=True), S.tok(disjoint=True)]
            t_ob = [S.tok() for _ in range(2)]
            pcn = {"i": 0}

            def nextp3():
                i = pcn["i"]
                pcn["i"] = (i + 1) % 6
                return pc[i], t_pc[i]

            def load_ft(ft):
                return [wload(win_d[72 + ft], key=("win", 72 + ft)), wload(wbc_d[ft], key=("wbc", ft)),
                        wload(win_d[80 + ft], key=("win", 80 + ft)), wload(wba_d[ft], nk=4, key=("wba", ft))]

            def xload(i):
                xb = i % 3
                S.add("sp", lambda e: e.dma_start(out=xr[xb][:], in_=x_d[16 + i]), writes=[t_xr[xb]], dma=True)

            def out_tile(i):
                b = i % 2
                xb = i % 3
                hf = i // 8
                for h in range(2):
                    unit(po[h][:, :], t_po[h], lambda kt: mg[:, kt, 128 * i:128 * i + 128],
                         lambda kt: wo[:, kt, 512 * h:512 * h + 512], 8, [t_mg[hf], t_wo])
                    S.add("dve", lambda e, h=h: e.tensor_tensor(out=tg[b][:, 512 * h:512 * h + 512], in0=po[h][:, :],
                                                                in1=gate[:, 512 * h:512 * h + 512], op=ALU.mult),
                          reads=[t_po[h]], writes=[t_tg[b]])
                S.add("pool", lambda e: e.tensor_tensor(out=ob[b][:], in0=tg[b][:], in1=xr[xb][:], op=ALU.add),
                      reads=[t_tg[b], t_xr[xb]], writes=[t_ob[b]])
                S.add("sp", lambda e: e.dma_start(out=out_d[i], in_=ob[b][:]), reads=[t_ob[b]], dma=True)
                if i + 3 < 16:
                    xload(i + 3)

            it = 0
            for half in range(2):
                Wn = load_ft(0)
                for ft in range(8):
                    (sGa, wGa), (sPa, wPa), (sGb, wGb), (sPb, wPb) = Wn
                    if ft + 1 < 8:
                        Wn = load_ft(ft + 1)
                    if half == 0 and ft == 1:
                        for kt in range(8):
                            S.add("pool", lambda e, kt=kt: e.dma_start(out=wo[:, kt, :], in_=wout_d[kt * 128:(kt + 1) * 128, :]),
                                  writes=[t_wo], dma=True)
                        for i in range(3):
                            xload(i)
                    for n in (2 * half, 2 * half + 1):
                        b = it % 2
                        it += 1
                        cs = slice(512 * n, 512 * n + 512)
                        pGa, tGa = nextp3()
                        unit(pGa[:, :], tGa, lambda kt: sGa[:, kt, :], lambda kt: hT[:, kt, cs], 8, [wGa])
                        S.add("act", lambda e, b=b, pGa=pGa: e.activation(out=tha[b][:], in_=pGa[:, :], func=AF.Tanh, scale=0.5),
                              reads=[tGa], writes=[t_tha[b]])
                        pA, tA = nextp3()
                        unit(pA[:, :], tA, lambda kt: sPa[:, kt, :], lambda kt: ya[:, kt, cs], 8, [wPa])
                        S.add("dve", lambda e, b=b, pA=pA: e.scalar_tensor_tensor(out=ma[b][:], in0=tha[b][:], scalar=1.0, in1=pA[:, :],
                                                                                  op0=ALU.add, op1=ALU.mult),
                              reads=[t_tha[b], tA], writes=[t_ma[b]])
                        pGb, tGb = nextp3()
                        unit(pGb[:, :], tGb, lambda kt: sGb[:, kt, :], lambda kt: hT[:, kt, cs], 8, [wGb])
                        S.add("act", lambda e, b=b, pGb=pGb: e.activation(out=tgb[b][:], in_=pGb[:, :], func=AF.Tanh, scale=0.5),
                              reads=[tGb], writes=[t_tgb[b]])
                        pBm, tBm = nextp3()
                        unit(pBm[:, :], tBm, lambda kt: sPb[:, kt, :], lambda kt: yb[:, kt, cs], 4, [wPb])
                        S.add("dve", lambda e, b=b, pBm=pBm: e.scalar_tensor_tensor(out=mb[b][:], in0=tgb[b][:], scalar=1.0, in1=pBm[:, :],
                                                                                    op0=ALU.add, op1=ALU.mult),
                              reads=[t_tgb[b], tBm], writes=[t_mb[b]])
                        S.add("pool", lambda e, b=b, ft=ft, cs=cs: e.tensor_tensor(out=mg[:, ft, cs], in0=ma[b][:], in1=mb[b][:], op=ALU.add),
                              reads=[t_ma[b], t_mb[b]], writes=[t_mg[half]])
                    if half == 1:
                        out_tile(ft)
            for i in range(8, 16):
                out_tile(i)
            S.emit(final=True)
    return nc


def _consts(core):
    q = core % 4
    hv = 1.0 if q > 0 else 0.0
    k = np.arange(128)[:, None]
    qq = np.arange(128)[None, :]
    D = (k <= qq).astype(np.float32)
    U = (k >= qq).astype(np.float32)
    hU = U * hv
    cst = np.zeros((128, NCST), np.float32)
    cst[:, 0:128] = np.eye(128, dtype=np.float32)
    bo = np.zeros((128, 128), np.float32)
    bo[0:64, 0:64] = 1.0 / 64
    bo[64:128, 64:128] = 1.0 / 64
    cst[:, 128:256] = bo
    cst[:, 256:320] = 2.0
    tiles = [np.concatenate([U, D, D, U], 1), np.concatenate([hU, D, D, U], 1), np.concatenate([D, U, D, U], 1),
             np.concatenate([U, D, U, D], 1), np.concatenate([hU, D, hU, D], 1)]
    for n in range(4):
        sl = slice(32 * n, 32 * n + 32)
        tiles.append(np.concatenate([hU[:, sl], D[:, sl]] * 8, 1))
    for i, t in enumerate(tiles):
        cst[:, 320 + 512 * i:320 + 512 * (i + 1)] = t
    return cst, hv


def _tile_w(w, nk):
    ncols = w.shape[1]
    return np.ascontiguousarray(w.reshape(nk, 128, ncols // 128, 128).transpose(2, 1, 0, 3))


_PROG = {}


def kernel(x, c, w_ada, b_ada, norm_w, w_in, conv_w, q_norm_w, k_norm_w, w_br_conv, w_br_attn, w_out):
    x = np.asarray(x, np.float32)
    c = np.asarray(c, np.float32)
    wada_t = _tile_w(np.asarray(w_ada, np.float32)[0], 8)
    win_t = _tile_w(np.asarray(w_in, np.float32)[0], 8)
    wbc_t = _tile_w(np.asarray(w_br_conv, np.float32)[0], 8)
    wba_t = _tile_w(np.asarray(w_br_attn, np.float32)[0], 4)
    wout = np.ascontiguousarray(np.asarray(w_out, np.float32)[0])
    rows = np.concatenate([np.asarray(b_ada, np.float32)[0], np.asarray(norm_w, np.float32)[0],
                           np.asarray(q_norm_w, np.float32)[0], np.asarray(k_norm_w, np.float32)[0]])[None, :]
    rows = np.ascontiguousarray(rows)
    cw = np.asarray(conv_w, np.float32)[0]
    in_maps = []
    for core in range(NCORES):
        b, q = core // 4, core % 4
        t0 = q * TOK
        cst, hv = _consts(core)
        xcat = np.zeros((2 * TOK, 1024), np.float32)
        if q > 0:
            xcat[0:TOK] = x[b, t0 - TOK:t0]
        xcat[TOK:] = x[b, t0:t0 + TOK]
        small = np.zeros((128, 32), np.float32)
        small[:, 0:24] = cw.reshape(3, 8, 128).transpose(2, 1, 0).reshape(128, 24)
        small[:, 24] = np.tile(np.asarray(q_norm_w, np.float32)[0], 2)
        small[:, 25] = np.tile(np.asarray(k_norm_w, np.float32)[0], 2)
        small[:, 26] = hv
        in_maps.append({
            "x": xcat.reshape(32, 128, 1024),
            "ccol": np.ascontiguousarray(c[b].reshape(8, 128).T),
            "cst": cst, "small": small, "rows": rows,
            "wada": wada_t, "win": win_t, "wbc": wbc_t, "wba": wba_t, "wout": wout,
        })
    if "nc" not in _PROG:
        _PROG["nc"] = build_program()
    res = run_bass_kernel_spmd(_PROG["nc"], in_maps[:NRUN], core_ids=list(range(NRUN)))
    if DEBUG:
        _PROG["res"] = res
    out = np.zeros((2, 4 * TOK, 1024), np.float32)
    for core in range(NRUN):
        b, q = core // 4, core % 4
        out[b, q * TOK:(q + 1) * TOK] = np.asarray(res.results[core]["out"]).reshape(TOK, 1024)
    return out
```
